# Optimizing a Trainium2 kernel written in Bass

```python
import jax, jax.numpy as jnp
from jax import lax
import numpy as np

D_MODEL = 1024
BATCH = 32
SEQ = 2048
DEPTH = 2

GRID_W = 64
CTX_LEN = 256
W_CONV = D_MODEL
W_LRU = D_MODEL
W_EVEN = W_CONV + W_LRU
N_LRU_HEADS = 8
LRU_HEAD_DIM = W_LRU // N_LRU_HEADS
CONV_WIDTH = 31
CONV_PAD = (15, 15)
SHORT_CONV_WIDTH = 4
SHORT_CONV_PAD = (2, 1)
LRU_C = 8.0
W_POOL = 2 * D_MODEL
POOL_WINDOWS = (2, 4, 8, 16)
N_POOL_GROUPS = len(POOL_WINDOWS)
POOL_GROUP_DIM = W_POOL // N_POOL_GROUPS
N_EVEN = (DEPTH + 1) // 2
N_ODD = DEPTH // 2
RMS_EPS = 1e-6
LN_EPS = 1e-5

kernel_name = "hybrid_conv_rglru_pool_prefix_dit"


def rmsnorm(x, g):
    xf = x.astype(jnp.float32)
    y = xf * lax.rsqrt(jnp.mean(xf * xf, axis=-1, keepdims=True) + RMS_EPS)
    return (y * g.astype(jnp.float32)).astype(x.dtype)


def layernorm(x, g, b):
    xf = x.astype(jnp.float32)
    mu = jnp.mean(xf, axis=-1, keepdims=True)
    var = jnp.mean(jnp.square(xf - mu), axis=-1, keepdims=True)
    y = (xf - mu) * lax.rsqrt(var + LN_EPS) * g.astype(jnp.float32) + b.astype(jnp.float32)
    return y.astype(x.dtype)


def modulate(h, shift, scale):
    return h * (1 + scale) + shift


def depthwise_conv(u, w, b, pad):
    y = lax.conv_general_dilated(
        u, w[:, None, :].astype(u.dtype), window_strides=(1,), padding=(pad,),
        dimension_numbers=("NWC", "WIO", "NWC"), feature_group_count=u.shape[-1])
    return y + b.astype(u.dtype)


def conformer_conv(va, vg, p):
    u = va * jax.nn.sigmoid(vg)
    u = depthwise_conv(u, p["conv_w"], p["conv_b"], CONV_PAD)
    return jax.nn.silu(layernorm(u, p["ln_g"], p["ln_b"]))


def lru_coeffs(u2, p):
    bn, n = u2.shape[0], u2.shape[1]
    uf = u2.astype(jnp.float32)
    uh = uf.reshape(bn, n, 2, N_LRU_HEADS, LRU_HEAD_DIM)
    gr = jnp.einsum("bnkhi,khij->bnkhj", uh, p["w_r"].astype(jnp.float32)).reshape(bn, n, 2, W_LRU)
    gi = jnp.einsum("bnkhi,khij->bnkhj", uh, p["w_i"].astype(jnp.float32)).reshape(bn, n, 2, W_LRU)
    r = jax.nn.sigmoid(gr + p["b_r"].astype(jnp.float32))
    i = jax.nn.sigmoid(gi + p["b_i"].astype(jnp.float32))
    log_a = -LRU_C * r * jax.nn.softplus(-p["lam"].astype(jnp.float32))
    a = jnp.exp(log_a)
    bx = jnp.sqrt(-jnp.expm1(2.0 * log_a)) * (i * uf)
    return a, bx


def lru_scan(a, bx, h0, with_outputs):
    def step(h, ab):
        at, bt = ab
        h = at * h + bt
        return h, (h if with_outputs else None)
    hT, hs = lax.scan(step, h0, (jnp.swapaxes(a, 0, 1), jnp.swapaxes(bx, 0, 1)))
    return hT, hs


def rglru_branch(xb, p, h0, with_outputs):
    u = depthwise_conv(xb, p["sconv_w"], p["sconv_b"], SHORT_CONV_PAD)
    u2 = jnp.stack([u, u[:, ::-1]], axis=2)
    a, bx = lru_coeffs(u2, p)
    hT, hs = lru_scan(a, bx, h0, with_outputs)
    if not with_outputs:
        return None, hT
    y = hs[:, :, 0] + hs[::-1, :, 1]
    return jnp.swapaxes(y, 0, 1).astype(xb.dtype), hT


def even_mixer(h, h0, p):
    z = h @ p["w_in"]
    va, vg, ga, xb, gb = jnp.split(
        z, [W_CONV, 2 * W_CONV, 3 * W_CONV, 3 * W_CONV + W_LRU], axis=-1)
    y_a = conformer_conv(va, vg, p) * jax.nn.silu(ga)
    y_b, hT = rglru_branch(xb, p, h0, True)
    y_b = y_b * jax.nn.silu(gb)
    return jnp.concatenate([y_a, y_b], axis=-1) @ p["w_out"], hT


def multiscale_pool(u):
    n = u.shape[1]
    uf = u.astype(jnp.float32)
    s = jnp.pad(jnp.cumsum(uf, axis=1), ((0, 0), (1, 0), (0, 0)))
    t = jnp.arange(n)
    parts = []
    for g, w in enumerate(POOL_WINDOWS):
        lo = jnp.clip(t - w // 2, 0, n)
        hi = jnp.clip(t + w // 2, 0, n)
        sg = s[:, :, g * POOL_GROUP_DIM:(g + 1) * POOL_GROUP_DIM]
        cnt = (hi - lo).astype(jnp.float32)[None, :, None]
        parts.append((sg[:, hi] - sg[:, lo]) / cnt)
    return (jnp.concatenate(parts, axis=-1) - uf).astype(u.dtype)


def odd_mixer(h, p, on_grid):
    bn, n = h.shape[0], h.shape[1]
    z = h @ p["w_in"]
    u, g = jnp.split(z, [W_POOL], axis=-1)
    if on_grid:
        rows = n // GRID_W
        d = multiscale_pool(u.reshape(bn * rows, GRID_W, W_POOL)).reshape(bn, n, W_POOL)
    else:
        d = multiscale_pool(u)
    d = d.reshape(bn, n, N_POOL_GROUPS, POOL_GROUP_DIM)
    y = jnp.einsum("bngi,gij->bngj", d, p["w_grp"]).reshape(bn, n, W_POOL) * p["scale"]
    return (y * jax.nn.silu(g)) @ p["w_out"]


def setup_inputs(seed: int = 0) -> dict:
    key = jax.random.key(seed)
    ks = jax.random.split(key, 32)
    f32 = jnp.float32

    def nrm(k, shape, s):
        return s * jax.random.normal(k, shape, f32)

    ua = jax.random.uniform(ks[20], (N_EVEN, 2, W_LRU), f32, 0.9, 0.999)
    pa = ua ** (1.0 / LRU_C)
    lam = jnp.log(pa) - jnp.log1p(-pa)
    return {
        "x": nrm(ks[0], (BATCH, SEQ, D_MODEL), 1.0),
        "c": nrm(ks[1], (BATCH, D_MODEL), 1.0),
        "ctx": nrm(ks[2], (BATCH, CTX_LEN, D_MODEL), 1.0),
        "c_ctx": nrm(ks[3], (D_MODEL,), 1.0),
        "norm_g": 1.0 + nrm(ks[4], (DEPTH, D_MODEL), 0.1),
        "mod_w": nrm(ks[5], (DEPTH, D_MODEL, 3 * D_MODEL), 0.5 * D_MODEL ** -0.5),
        "mod_b": nrm(ks[6], (DEPTH, 3 * D_MODEL), 0.02),
        "ev_w_in": nrm(ks[7], (N_EVEN, D_MODEL, 3 * W_CONV + 2 * W_LRU), D_MODEL ** -0.5),
        "ev_conv_w": nrm(ks[8], (N_EVEN, CONV_WIDTH, W_CONV), CONV_WIDTH ** -0.5),
        "ev_conv_b": nrm(ks[9], (N_EVEN, W_CONV), 0.02),
        "ev_ln_g": 1.0 + nrm(ks[10], (N_EVEN, W_CONV), 0.1),
        "ev_ln_b": nrm(ks[11], (N_EVEN, W_CONV), 0.02),
        "ev_sconv_w": nrm(ks[12], (N_EVEN, SHORT_CONV_WIDTH, W_LRU), SHORT_CONV_WIDTH ** -0.5),
        "ev_sconv_b": nrm(ks[13], (N_EVEN, W_LRU), 0.02),
        "ev_w_r": nrm(ks[14], (N_EVEN, 2, N_LRU_HEADS, LRU_HEAD_DIM, LRU_HEAD_DIM), LRU_HEAD_DIM ** -0.5),
        "ev_b_r": nrm(ks[15], (N_EVEN, 2, W_LRU), 0.02),
        "ev_w_i": nrm(ks[16], (N_EVEN, 2, N_LRU_HEADS, LRU_HEAD_DIM, LRU_HEAD_DIM), LRU_HEAD_DIM ** -0.5),
        "ev_b_i": nrm(ks[17], (N_EVEN, 2, W_LRU), 0.02),
        "ev_lam": lam,
        "ev_w_out": nrm(ks[18], (N_EVEN, W_EVEN, D_MODEL), W_EVEN ** -0.5),
        "od_w_in": nrm(ks[21], (N_ODD, D_MODEL, 2 * W_POOL), D_MODEL ** -0.5),
        "od_w_grp": nrm(ks[22], (N_ODD, N_POOL_GROUPS, POOL_GROUP_DIM, POOL_GROUP_DIM), POOL_GROUP_DIM ** -0.5),
        "od_scale": 1.0 + nrm(ks[23], (N_ODD, W_POOL), 0.1),
        "od_w_out": nrm(ks[24], (N_ODD, W_POOL, D_MODEL), W_POOL ** -0.5),
        "final_g": 1.0 + nrm(ks[25], (D_MODEL,), 0.1),
    }


def reference(x, c, ctx, c_ctx, norm_g, mod_w, mod_b,
              ev_w_in, ev_conv_w, ev_conv_b, ev_ln_g, ev_ln_b, ev_sconv_w, ev_sconv_b,
              ev_w_r, ev_b_r, ev_w_i, ev_b_i, ev_lam, ev_w_out,
              od_w_in, od_w_grp, od_scale, od_w_out, final_g):
    bn = x.shape[0]
    sc = jax.nn.silu(c)
    scc = jax.nn.silu(c_ctx)
    for l in range(DEPTH):
        ctx_out_needed = any(j % 2 == 0 for j in range(l + 1, DEPTH))
        shift, scale, gate = jnp.split(sc @ mod_w[l] + mod_b[l], 3, axis=-1)
        hx = modulate(rmsnorm(x, norm_g[l]), shift[:, None], scale[:, None])
        if ctx_out_needed or l % 2 == 0:
            shift_c, scale_c, gate_c = jnp.split(scc @ mod_w[l] + mod_b[l], 3, axis=-1)
            hc = modulate(rmsnorm(ctx, norm_g[l]), shift_c, scale_c)
        j = l // 2
        if l % 2 == 0:
            pe = {"w_in": ev_w_in[j], "conv_w": ev_conv_w[j], "conv_b": ev_conv_b[j],
                  "ln_g": ev_ln_g[j], "ln_b": ev_ln_b[j], "sconv_w": ev_sconv_w[j],
                  "sconv_b": ev_sconv_b[j], "w_r": ev_w_r[j], "b_r": ev_b_r[j],
                  "w_i": ev_w_i[j], "b_i": ev_b_i[j], "lam": ev_lam[j], "w_out": ev_w_out[j]}
            h0 = jnp.zeros((bn, 2, W_LRU), jnp.float32)
            if ctx_out_needed:
                out_c, h_ctx = even_mixer(hc, h0, pe)
            else:
                xb_c = hc @ pe["w_in"][:, 3 * W_CONV:3 * W_CONV + W_LRU]
                _, h_ctx = rglru_branch(xb_c, pe, h0, False)
            out_x, _ = even_mixer(hx, h_ctx, pe)
        else:
            po = {"w_in": od_w_in[j], "w_grp": od_w_grp[j], "scale": od_scale[j], "w_out": od_w_out[j]}
            if ctx_out_needed:
                out_c = odd_mixer(hc, po, False)
            out_x = odd_mixer(hx, po, True)
        x = x + gate[:, None] * out_x
        if ctx_out_needed:
            ctx = ctx + gate_c * out_c
    return rmsnorm(x, final_g)
```

```python
import math
import numpy as np
import concourse.bass as bass
import concourse.mybir as mybir
from concourse.bass_utils import run_bass_kernel_spmd

F32 = mybir.dt.float32
BF16 = mybir.dt.bfloat16
AF = mybir.ActivationFunctionType
ALU = mybir.AluOpType

NB = 4
T = 2048
D = 1024
CTX = 256
NT = 4
RMS_EPS = 1e-6
LN_EPS = 1e-5
LN025 = math.log(0.25)
POOL_W = (2, 4, 8, 16)

R_C, R_CCTX, R_NG, R_MODB, R_CW, R_CB, R_LG, R_LB, R_SW, R_SB, R_BR, R_BI, R_LAM, R_OS, R_FG = \
    0, 4, 5, 7, 13, 44, 45, 46, 47, 51, 52, 54, 56, 58, 60
NROW = 64
WLA = 1
GATE_PREFETCH = False
SPLIT_RELOAD = True
SPLIT_Q0_INTERLEAVE = True


class Ev:
    __slots__ = ("sem", "val", "dma")

    def __init__(self, sem, val, dma):
        self.sem, self.val, self.dma = sem, val, dma


class Tracker:
    def __init__(self, nc):
        self.nc = nc
        self.engs = dict(pe=nc.tensor, act=nc.scalar, dve=nc.vector, pool=nc.gpsimd, sp=nc.sync)
        self.esem = {e: nc.alloc_semaphore("c_" + e) for e in ("pe", "act", "dve", "pool")}
        self.ecnt = dict.fromkeys(self.esem, 0)
        self.waited = {}
        self.W = {}
        self.R = {}
        self.children = {}
        self.pend_r, self.pend_w = [], []
        self.dsem = {}
        self.dcnt = {}
        self.psi = 0
        self.nwait = 0

    def _rel(self, key):
        if isinstance(key, tuple) and key[0] == "scr":
            if len(key) == 3:
                par = key[:2]
                self.children.setdefault(par, set()).add(key)
                return (key, par)
            return (key,) + tuple(self.children.get(key, ()))
        return (key,)

    def _wait(self, eng, ev):
        val = self.dcnt[ev.sem.num] if ev.dma else ev.val
        kk = (eng, ev.sem.num)
        if self.waited.get(kk, 0) >= val:
            return
        self.engs[eng].wait_ge(ev.sem, val)
        self.waited[kk] = val
        self.nwait += 1

    def deps(self, eng, reads, writes):
        for key in reads:
            for k2 in self._rel(key):
                ev = self.W.get(k2)
                if ev is not None:
                    self._wait(eng, ev)
        for key in writes:
            for k2 in self._rel(key):
                ev = self.W.get(k2)
                if ev is not None:
                    self._wait(eng, ev)
                for ev in self.R.get(k2, {}).values():
                    self._wait(eng, ev)

    def fence(self, engs=("act", "dve"), nscr=5):
        for e in engs:
            self.deps(e, [], [("scr", i) for i in range(nscr)])

    def retire(self, keys, engs=("act", "dve", "pool")):
        for e in engs:
            self.deps(e, [], keys)

    def commit(self, ev, reads, writes):
        for key in reads:
            d = self.R.setdefault(key, {})
            d[ev.sem.num] = ev
        for key in writes:
            self.W[key] = ev
            self.R[key] = {}

    def op(self, eng, fn, reads, writes, **kw):
        self.deps(eng, reads, writes)
        ins = fn(**kw)
        self.ecnt[eng] += 1
        ins.then_inc(self.esem[eng], 1)
        self.commit(Ev(self.esem[eng], self.ecnt[eng], False), reads, writes)
        return ins

    def mm(self, fn, reads, writes, signal, **kw):
        self.deps("pe", reads, writes)
        ins = fn(**kw)
        self.pend_r += list(reads)
        self.pend_w += list(writes)
        if signal:
            self.ecnt["pe"] += 1
            ins.then_inc(self.esem["pe"], 1)
            self.commit(Ev(self.esem["pe"], self.ecnt["pe"], False), self.pend_r, self.pend_w)
            self.pend_r, self.pend_w = [], []
        return ins

    def dma(self, q, semname, out, in_, reads, writes):
        if semname not in self.dsem:
            self.dsem[semname] = self.nc.alloc_semaphore("d_" + semname)
            self.dcnt[self.dsem[semname].num] = 0
        sem = self.dsem[semname]
        self.deps(q, reads, writes)
        ins = self.engs[q].dma_start(out=out, in_=in_)
        ins.then_inc(sem, 16)
        self.dcnt[sem.num] += 16
        self.commit(Ev(sem, self.dcnt[sem.num], True), reads, writes)
        return ins


def build_program():
    nc = bass.Bass("TRN2", target_bir_lowering=False)
    dt_in = lambda name, shape: nc.dram_tensor(name, shape, F32, kind="ExternalInput").ap()
    x_d = dt_in("x", [NB, T, D])
    ctx_d = dt_in("ctx", [NB, CTX, D])
    prm_d = dt_in("prm", [NROW, D])
    modw_d = dt_in("mod_w", [2, D, 3 * D])
    win0_d = dt_in("ev_w_in", [D, 5 * D])
    wri_d = dt_in("ev_w_ri", [2, 2, 8, 128, 128])
    wout0_d = dt_in("ev_w_out", [2 * D, D])
    win1_d = dt_in("od_w_in", [D, 4 * D])
    wgrp_d = dt_in("od_w_grp", [4, 512, 512])
    wout1_d = dt_in("od_w_out", [2 * D, D])
    out_d = nc.dram_tensor("out", [NB, T, D], F32, kind="ExternalOutput").ap()

    k = Tracker(nc)
    V, A, P, G = nc.vector, nc.scalar, nc.tensor, nc.gpsimd

    SCRW = 2056
    sizes = [("R1", 4 * T), ("R2", 4 * T), ("R3", 4 * T), ("R4", 4 * T), ("hcT", 4 * CTX), ("scr", 5 * SCRW), ("ubf", T // 2),
             ("ring", 4 * 1024), ("gw", 2 * 256), ("prmT", 8 * NROW), ("modT", 256), ("gsT", 128), ("lc", 128),
             ("hctx", 16), ("scT", 32), ("misc", 64), ("ident", 128), ("ones", 64), ("icnt", 4 * 64), ("wic", 4 * 64),
             ("xbpb", 1032), ("sdiag", 512)]
    off = {}
    tot = 0
    for n_, s_ in sizes:
        off[n_] = tot
        tot += s_
    arena = nc.alloc_sbuf_tensor("arena", [128, tot], F32)

    def fv(name, n, o=0):
        return arena[:, off[name] + o: off[name] + o + n]

    quadf = lambda r: fv(r, 4 * T).rearrange("p (k t) -> p k t", k=4)
    octb = lambda r: fv(r, 4 * T).bitcast(BF16).rearrange("p (k t) -> p k t", k=8)

    class Res:
        def __init__(self, ra, rb, pfx):
            self.q = [quadf(ra), quadf(rb)]
            self.pfx = pfx

        def c(self, kc, sl):
            return self.q[kc // 4][:, kc % 4, sl]

        def key(self, kc, t):
            return (self.pfx, kc, t)

    X0 = Res("R1", "R2", "x0")
    X1 = Res("R2", "R3", "x1")
    H0v, H1v = octb("R3"), octb("R1")
    Ybv, Ycv = octb("R1"), octb("R4")
    ALLK = lambda p, ks=range(8): [(p, a, t) for a in ks for t in range(NT)]
    r2 = fv("R2", 4 * T)
    upad = r2[:, 0:1040].bitcast(BF16)[:, 0:T + 30]
    diagb = [r2[:, 1040 + i * 1984:1040 + (i + 1) * 1984].bitcast(BF16).rearrange("p (k m) -> p k m", k=31) for i in range(2)]
    sgt = [r2[:, 5008:5520], r2[:, 5520:6032]]
    ringBf = [r2[:, 6032:7056].bitcast(BF16), r2[:, 7056:8080].bitcast(BF16), fv("hcT", 4 * CTX).bitcast(BF16)]
    R2K = ["upad", ("diag", 0), ("diag", 1), "sg0", "sg1", ("ringB", 0), ("ringB", 1)]
    hcT = fv("hcT", 4 * CTX).bitcast(BF16).rearrange("p (k t) -> p k t", k=8)
    scr = [fv("scr", SCRW, i * SCRW) for i in range(5)]
    scrb = [s.bitcast(BF16) for s in scr]
    ubf = fv("ubf", T // 2).bitcast(BF16)
    ringf = [fv("ring", 1024, i * 1024).bitcast(BF16) for i in range(4)]
    gwf = [fv("gw", 256, i * 256).bitcast(BF16).rearrange("p (g m) -> p g m", g=4) for i in range(2)]
    prmT = fv("prmT", 8 * NROW).rearrange("p (k r) -> p k r", k=8)
    modT = fv("modT", 240).rearrange("p (l m b) -> p l m b", l=2, m=24)
    gsT = fv("gsT", 80).rearrange("p (l k b) -> p l k b", l=2, k=8)
    lc = fv("lc", 96).rearrange("p (c j d) -> p c j d", c=6, j=8)
    hctx = fv("hctx", 16).rearrange("p (j d) -> p j d", j=8)
    scT = fv("scT", 20).bitcast(BF16).rearrange("p (k b) -> p k b", k=8)
    misc = fv("misc", 64)
    onecol = misc[:, 0:1]
    hsc = misc[:, 8:24].rearrange("p (g k) -> p g k", g=2)
    hlg = misc[:, 24:32]
    hlb = misc[:, 32:40]
    ident = fv("ident", 128)
    ones_bf = fv("ones", 64).bitcast(BF16)
    icnt = fv("icnt", 256).rearrange("p (g t) -> p g t", g=4)
    wic = fv("wic", 256).rearrange("p (g t) -> p g t", g=4)
    xbpb = fv("xbpb", 1032).bitcast(BF16)
    sdiag = fv("sdiag", 512).bitcast(BF16).rearrange("p (s k m) -> p s k m", s=2, k=4)

    psall = nc.alloc_psum_tensor("psall", [128, 8 * 512], F32)

    def psum():
        i = k.psi % 8
        k.psi += 1
        return psall[:, i * 512:(i + 1) * 512], ("ps", i)

    def bc_ap(ap, n):
        return bass.AP(ap.tensor, ap.offset, [list(ap.ap[0]), [0, n]])

    def rev_ap(ap):
        n = ap.shape[1]
        return bass.AP(ap.tensor, ap.offset + (n - 1), [list(ap.ap[0]), [-1, n]])

    def rows_ap(base, o, n, bcast=False):
        a = base[:, o:o + 1]
        return bass.AP(a.tensor, a.offset, [list(a.ap[0]), [64, 32], [0 if bcast else 1, n]])

    def wblk(w2d, r0, c0, ncol=256, nk=8):
        return w2d[r0:r0 + nk * 128, c0:c0 + ncol].rearrange("(kc p) m -> p kc m", p=128)

    plan = []
    for l in range(2):
        for blk in range(12):
            plan.append((wblk(modw_d[l], 0, blk * 256), (8, 256)))
    for bi in range(NB):
        for blk in range(4):
            plan.append((wblk(win0_d, 0, 3 * D + blk * 256), (8, 256)))
        for blk in range(4):
            plan.append((wblk(win0_d, 0, 3 * D + blk * 256), (8, 256)))
            plan.append((wblk(win0_d, 0, 4 * D + blk * 256), (8, 256)))
        for blk in range(4):
            plan.append((wblk(win0_d, 0, 2 * D + blk * 256), (8, 256)))
        for mblk in range(4):
            plan.append((wblk(wout0_d, D, mblk * 256), (8, 256)))
        for mblk in range(4):
            plan.append((wblk(wout0_d, 0, mblk * 256), (8, 256)))
        for hf in range(2):
            for blk in range(4):
                plan.append((wblk(win1_d, 0, hf * D + blk * 256), (8, 256)))
            for gl in range(2):
                g = hf * 2 + gl
                plan.append((wgrp_d[g].rearrange("(ki p) m -> p ki m", p=128), (4, 512)))
                plan.append((wblk(win1_d, 0, 2 * D + g * 512), (8, 256)))
                plan.append((wblk(win1_d, 0, 2 * D + g * 512 + 256), (8, 256)))
            for mblk in range(4):
                plan.append((wblk(wout1_d, hf * D, mblk * 256), (8, 256)))
    wq = {"used": 0, "issued": 0, "views": []}
    planB = []
    for bi in range(NB):
        for blk in range(4):
            planB.append(wblk(win0_d, 0, blk * 256))
            planB.append(wblk(win0_d, 0, D + blk * 256))
    wqB = {"used": 0, "issued": 0, "views": []}

    def wgetB():
        n = wqB["used"]
        wqB["used"] += 1
        while wqB["issued"] < min(n + 2, (n // 8 + 1) * 8):
            i = wqB["issued"]
            wqB["issued"] += 1
            view = ringBf[i % 3][:, 0:2048].rearrange("p (a b) -> p a b", a=8)
            k.dma("pool", "ringB%d" % (i % 3), out=view, in_=planB[i], reads=[], writes=[("ringB", i % 3)])
            wqB["views"].append((view, ("ringB", i % 3)))
        return wqB["views"][n]

    def wget(la=None):
        la = WLA if la is None else la
        n = wq["used"]
        wq["used"] += 1
        while wq["issued"] < min(n + la + 1, len(plan)):
            i = wq["issued"]
            wq["issued"] += 1
            dram_ap, (a_, b_) = plan[i]
            view = ringf[i % 4][:, 0:a_ * b_].rearrange("p (a b) -> p a b", a=a_)
            k.dma("pool", "ring%d" % (i % 4), out=view, in_=dram_ap, reads=[], writes=[("ring", i % 4)])
            wq["views"].append((view, ("ring", i % 4)))
        return wq["views"][n]

    k.op("pool", G.memset, [], ["ident"], ap=ident, constant=0.0)
    k.op("pool", G.affine_select, ["ident"], ["ident"], out=ident, in_=ident, pattern=[[-1, 128]],
         compare_op=ALU.not_equal, fill=1.0, base=0, channel_multiplier=1)
    k.op("pool", G.memset, [], ["ones"], ap=ones_bf, constant=1.0)
    k.op("pool", G.memset, [], ["misc"], ap=onecol, constant=1.0)
    for g, w in enumerate(POOL_W):
        k.op("pool", G.memset, [], ["icnt"], ap=icnt[:, g, :], constant=1.0 / w)
        for t in range(w // 2):
            k.op("pool", G.memset, [], ["icnt"], ap=icnt[:, g, t:t + 1], constant=1.0 / (t + w // 2))
        for t in range(64 - w // 2 + 1, 64):
            k.op("pool", G.memset, [], ["icnt"], ap=icnt[:, g, t:t + 1], constant=1.0 / (64 - t + w // 2))

    for g, w in enumerate(POOL_W):
        k.op("pool", G.memset, [], ["icnt"], ap=wic[:, g, :], constant=1.0)
        for t in range(w // 2):
            k.op("pool", G.memset, [], ["icnt"], ap=wic[:, g, t:t + 1], constant=float(w) / (t + w // 2))
        for t in range(64 - w // 2 + 1, 64):
            k.op("pool", G.memset, [], ["icnt"], ap=wic[:, g, t:t + 1], constant=float(w) / (64 - t + w // 2))

    k.op("pool", G.memset, [], ["xbpad"], ap=xbpb[:, 0:2], constant=0.0)
    k.op("pool", G.memset, [], ["xbpad"], ap=xbpb[:, 2 + T:3 + T], constant=0.0)

    k.dma("sp", "prm", out=scr[0][0:NROW, 0:D], in_=prm_d, reads=[], writes=[("scr", 0)])
    for kc in range(8):
        ps, pk = psum()
        k.mm(P.transpose, [("scr", 0), "ident"], [pk], True, out=ps[:, 0:NROW],
             in_=scr[0][0:NROW, kc * 128:(kc + 1) * 128], identity=ident[0:NROW, 0:NROW])
        k.op("act", A.copy, [pk], ["prm"], out=prmT[:, kc, :], in_=ps[:, 0:NROW])
    PR = ["prm"]

    t5a = scr[1][:, 0:40].rearrange("p (k b) -> p k b", k=8)
    k.op("act", A.activation, PR, [("scr", 1)], out=t5a, in_=prmT[:, :, 0:5], func=AF.Tanh, scale=0.5)
    k.op("dve", V.scalar_tensor_tensor, PR + [("scr", 1)], [("scr", 1)], out=t5a, in0=t5a, scalar=1.0,
         in1=prmT[:, :, 0:5], op0=ALU.add, op1=ALU.mult)
    k.op("dve", V.tensor_scalar, [("scr", 1)], ["scT"], out=scT, in0=t5a, scalar1=0.5, scalar2=None, op0=ALU.mult)

    for l in range(2):
        for blk in range(12):
            wv, wk = wget()
            for mm_ in range(2):
                m = blk * 2 + mm_
                ps, pk = psum()
                for kc in range(8):
                    k.mm(P.matmul, [wk, "scT"], [pk], kc == 7, out=ps[:, 0:5], lhsT=wv[:, kc, mm_ * 128:(mm_ + 1) * 128],
                         rhs=scT[:, kc, :], start=(kc == 0), stop=(kc == 7))
                row = R_MODB + 3 * l + m // 8
                k.op("dve", V.tensor_scalar, [pk] + PR, ["mod"], out=modT[:, l, m, :], in0=ps[:, 0:5],
                     scalar1=prmT[:, m % 8, row:row + 1], scalar2=None, op0=ALU.add)
        for kc in range(8):
            k.op("dve", V.tensor_scalar, ["mod"] + PR, ["mod"], out=gsT[:, l, kc, :], in0=modT[:, l, 8 + kc, :],
                 scalar1=1.0, scalar2=prmT[:, kc, R_NG + l:R_NG + l + 1], op0=ALU.add, op1=ALU.mult)
    MOD = ["mod"]

    lam = prmT[:, :, R_LAM:R_LAM + 2]
    s1 = scr[1]
    tv = lambda i: s1[:, 64 + 16 * i:64 + 16 * i + 16].rearrange("p (j d) -> p j d", j=8)
    S1 = [("scr", 1)]
    k.op("dve", V.tensor_scalar, PR, ["lc"], out=lc[:, 0], in0=prmT[:, :, R_BR:R_BR + 2], scalar1=0.5, scalar2=None, op0=ALU.mult)
    k.op("dve", V.tensor_scalar, PR, ["lc"], out=lc[:, 1], in0=prmT[:, :, R_BI:R_BI + 2], scalar1=0.5, scalar2=None, op0=ALU.mult)
    k.op("act", A.activation, PR, S1, out=tv(0), in_=lam, func=AF.Abs)
    k.op("act", A.activation, S1, S1, out=tv(0), in_=tv(0), func=AF.Exp, scale=-1.0)
    k.op("dve", V.tensor_scalar, S1, S1, out=tv(1), in0=tv(0), scalar1=2.0, scalar2=None, op0=ALU.add)
    k.op("dve", V.reciprocal, S1, S1, out=tv(1), in_=tv(1))
    k.op("dve", V.tensor_tensor, S1, S1, out=tv(1), in0=tv(1), in1=tv(0), op=ALU.mult)
    k.op("dve", V.tensor_tensor, S1, S1, out=tv(2), in0=tv(1), in1=tv(1), op=ALU.mult)
    k.op("dve", V.memset, [], S1, ap=tv(3), constant=1.0 / 15.0)
    for cc in (13.0, 11.0, 9.0, 7.0, 5.0, 3.0, 1.0):
        k.op("dve", V.tensor_tensor, S1, S1, out=tv(3), in0=tv(3), in1=tv(2), op=ALU.mult)
        k.op("dve", V.tensor_scalar, S1, S1, out=tv(3), in0=tv(3), scalar1=1.0 / cc, scalar2=None, op0=ALU.add)
    k.op("dve", V.tensor_tensor, S1, S1, out=tv(3), in0=tv(3), in1=tv(1), op=ALU.mult)
    k.op("dve", V.tensor_scalar, PR, S1, out=tv(4), in0=lam, scalar1=-1.0, scalar2=0.0, op0=ALU.mult, op1=ALU.max)
    k.op("dve", V.scalar_tensor_tensor, S1, S1, out=tv(4), in0=tv(3), scalar=2.0, in1=tv(4), op0=ALU.mult, op1=ALU.add)
    k.op("dve", V.tensor_scalar, S1, ["lc"], out=lc[:, 3], in0=tv(4), scalar1=-8.0, scalar2=None, op0=ALU.mult)
    k.op("dve", V.tensor_scalar, S1, ["lc"], out=lc[:, 2], in0=tv(4), scalar1=-4.0, scalar2=None, op0=ALU.mult)
    k.op("dve", V.tensor_scalar, S1, ["lc"], out=lc[:, 4], in0=tv(4), scalar1=-8.0, scalar2=LN025, op0=ALU.mult, op1=ALU.add)
    k.op("dve", V.tensor_scalar, PR, ["lc"], out=hsc, in0=prmT[:, :, R_OS:R_OS + 2].rearrange("p k g -> p g k"),
         scalar1=0.5, scalar2=None, op0=ALU.mult)
    k.op("dve", V.tensor_scalar, PR, ["lc"], out=hlg, in0=prmT[:, :, R_LG], scalar1=0.5, scalar2=None, op0=ALU.mult)
    k.op("dve", V.tensor_scalar, PR, ["lc"], out=hlb, in0=prmT[:, :, R_LB], scalar1=0.5, scalar2=None, op0=ALU.mult)
    LC = ["lc"]

    tsl = lambda t: slice(t * 512, (t + 1) * 512)

    def load_transposed(src_rows, ntok_tiles, dstq, dkey):
        for tt in range(ntok_tiles):
            si = tt % 2
            st = scr[si][:, 0:D]
            k.dma("sp", "xst%d" % si, out=st, in_=src_rows(tt), reads=[], writes=[("scr", si)])
            for h in range(2):
                ps, pk = psum()
                for q in range(4):
                    kk = h * 4 + q
                    k.mm(P.transpose, [("scr", si), "ident"], [pk], q == 3, out=ps[:, q * 128:(q + 1) * 128],
                         in_=st[:, kk * 128:(kk + 1) * 128], identity=ident)
                k.op("act", A.copy, [pk], [dkey(kk_, tt) for kk_ in range(h * 4, h * 4 + 4)],
                     out=dstq(h, tt), in_=ps.rearrange("p (q t) -> p q t", q=4))

    SM = lambda name: ("scr", 4, name)
    rs_t = scr[4][:, 0:512]
    tmp_t = [scr[4][:, 512:1024], scr[4][:, 1024:1536]]
    sqb_t = [scrb[4][:, 3072:3584], scrb[4][:, 3584:4096]]

    def rstd_tile(src, skey, n, out_rs):
        ps, pk = psum()
        for kc in range(8):
            sq = sqb_t[kc % 2][:, 0:n]
            k.op("act", A.activation, [skey(kc)], [SM("sq%d" % (kc % 2))], out=sq, in_=src(kc), func=AF.Square)
            k.mm(P.matmul, [SM("sq%d" % (kc % 2)), "ones"], [pk], True, out=ps[:, 0:n], lhsT=ones_bf, rhs=sq,
                 start=(kc == 0), stop=(kc == 7))
        k.op("act", A.activation, [pk, "misc"], [SM("rs")], out=out_rs, in_=ps[:, 0:n], func=AF.Sqrt, scale=1.0 / D, bias=epsr)
        k.op("dve", V.reciprocal, [SM("rs")], [SM("rs")], out=out_rs, in_=out_rs)

    def modulated_norm(src, skey, n, l, b, dst, dkey):
        rs = rs_t[:, 0:n]
        rstd_tile(src, skey, n, rs)
        for kc in range(8):
            tm = tmp_t[kc % 2][:, 0:n]
            k.op("dve", V.scalar_tensor_tensor, [skey(kc), SM("rs")] + MOD, [SM("tm%d" % (kc % 2))], out=tm, in0=src(kc),
                 scalar=gsT[:, l, kc, b:b + 1], in1=rs, op0=ALU.mult, op1=ALU.mult)
            k.op("act", A.activation, [SM("tm%d" % (kc % 2))] + MOD, [dkey(kc)], out=dst(kc), in_=tm, func=AF.Identity,
                 bias=modT[:, l, kc, b:b + 1])

    epsr = misc[:, 1:2]
    epsl = misc[:, 2:3]
    k.op("pool", G.memset, [], ["misc"], ap=epsr, constant=RMS_EPS)
    k.op("pool", G.memset, [], ["misc"], ap=epsl, constant=LN_EPS)
    halfc, nhalfc, q25c = misc[:, 3:4], misc[:, 4:5], misc[:, 5:6]
    k.op("pool", G.memset, [], ["misc"], ap=halfc, constant=0.5)
    k.op("pool", G.memset, [], ["misc"], ap=nhalfc, constant=-0.5)
    k.op("pool", G.memset, [], ["misc"], ap=q25c, constant=0.25)

    def out_proj(l, b, res, yv, ypfx, mblks=range(4), hook=None):
        step = 0
        for mblk in mblks:
            wv, wk = wget()
            for mm_ in range(2):
                m = mblk * 2 + mm_
                for t in range(NT):
                    if hook is not None:
                        hook(step)
                    step += 1
                    ps, pk = psum()
                    for kc in range(8):
                        k.mm(P.matmul, [wk, (ypfx, kc, t)], [pk], kc == 7, out=ps, lhsT=wv[:, kc, mm_ * 128:(mm_ + 1) * 128],
                             rhs=yv[:, kc, tsl(t)], start=(kc == 0), stop=(kc == 7))
                    k.op("dve", V.scalar_tensor_tensor, [pk, res.key(m, t)] + MOD, [res.key(m, t)], out=res.c(m, tsl(t)), in0=ps,
                         scalar=modT[:, l, 16 + m, b:b + 1], in1=res.c(m, tsl(t)), op0=ALU.mult, op1=ALU.add)

    def out_proj_touter(l, b, res, yv, ypfx, after_tile):
        blocks = [wget(), wget(), wget(), wget(0)]
        for t in range(NT):
            for m in range(8):
                wv, wk = blocks[m // 2]
                mm_ = m % 2
                ps, pk = psum()
                for kc in range(8):
                    k.mm(P.matmul, [wk, (ypfx, kc, t)], [pk], kc == 7, out=ps, lhsT=wv[:, kc, mm_ * 128:(mm_ + 1) * 128],
                         rhs=yv[:, kc, tsl(t)], start=(kc == 0), stop=(kc == 7))
                k.op("dve", V.scalar_tensor_tensor, [pk, res.key(m, t)] + MOD, [res.key(m, t)], out=res.c(m, tsl(t)), in0=ps,
                     scalar=modT[:, l, 16 + m, b:b + 1], in1=res.c(m, tsl(t)), op0=ALU.mult, op1=ALU.add)
            after_tile(t)

    def reload_tile(bi, tt, h, stv, stk, semname):
        k.dma("sp", semname, out=stv, in_=x_d[bi, tt * 128:(tt + 1) * 128, h * 512:(h + 1) * 512], reads=[], writes=[stk])
        ps, pk = psum()
        for q in range(4):
            k.mm(P.transpose, [stk, "ident"], [pk], q == 3, out=ps[:, q * 128:(q + 1) * 128],
                 in_=stv[:, q * 128:(q + 1) * 128], identity=ident)
        k.op("act", A.copy, [pk], [X1.key(h * 4 + q, tt // 4) for q in range(4)],
             out=X1.q[h][:, :, tt * 128:(tt + 1) * 128], in_=ps.rearrange("p (q t) -> p q t", q=4))

    B1, B2, B4, B5, B6 = scr[0], scr[1], scr[2], scr[3], scr[4]
    K1, K2, K4, K5, K6 = [("scr", i) for i in range(5)]

    sw_ = lambda j, i: prmT[:, j, R_SW + i:R_SW + i + 1]
    HSb = [B5, B1]
    HSK = lambda d, h: ("scr", 3 if d == 0 else 0, "h%d" % h)

    def build_sdiag(j):
        sd = sdiag[:, j % 2]
        for i in range(4):
            k.op("dve", V.tensor_scalar, ["ident"] + PR, [("sdiag", j % 2)], out=sd[:, i, :], in0=ident, scalar1=sw_(j, i),
                 scalar2=None, op0=ALU.mult)
        return sd, ("sdiag", j % 2)

    def lru_head(j, jj, wA, kA):
        sd, sdk = build_sdiag(j)
        for t in range(NT):
            ps, pk = psum()
            for kc in range(8):
                k.mm(P.matmul, [kA, ("h0", kc, t)], [pk], kc == 7, out=ps, lhsT=wA[:, kc, jj * 128:(jj + 1) * 128],
                     rhs=H0v[:, kc, tsl(t)], start=(kc == 0), stop=(kc == 7))
            k.op("act", A.copy, [pk], [("xbpb", t)], out=xbpb[:, 2 + t * 512:2 + (t + 1) * 512], in_=ps)
        for t in range(NT):
            ps, pk = psum()
            rk = [("xbpb", tt) for tt in (t - 1, t, t + 1) if 0 <= tt < NT] + [sdk, "xbpad"]
            for i in range(4):
                k.mm(P.matmul, rk, [pk], i == 3, out=ps, lhsT=sd[:, i, :], rhs=xbpb[:, t * 512 + i:t * 512 + i + 512],
                     start=(i == 0), stop=(i == 3))
            k.op("act", A.activation, [pk] + PR, [("scr", 1, "u%d" % t)], out=B2[:, tsl(t)], in_=ps, func=AF.Identity,
                 bias=prmT[:, j, R_SB:R_SB + 1])
        k.op("dve", V.tensor_copy, [K2], ["ubf"], out=ubf[:, 0:T], in_=B2[:, 0:T])

    LST = [(0, 0), (0, 1), (1, 1), (1, 0)]

    def lbufs(si):
        d, h = LST[si]
        st = si % 2
        return (d, h, B4[:, st * 1024:(st + 1) * 1024], ("scr", 2, "s%d" % st), B6[:, st * 1024:(st + 1) * 1024],
                ("scr", 4, "s%d" % st), HSb[d][:, h * 1024:(h + 1) * 1024], HSK(d, h))

    def gate_front(j, d, gwv, gwk, ubv, ubk, c0, tw, ntl, TR, KTR, E, KE, hsv, KH, uv, uk):
        for tl in range(ntl):
            ps, pk = psum()
            k.mm(P.matmul, [gwk, ubk], [pk], True, out=ps[:, 0:tw], lhsT=gwv[:, d, :], rhs=ubv[:, c0 + tl * tw:c0 + (tl + 1) * tw],
                 start=True, stop=True)
            k.op("act", A.activation, [pk] + LC, [KTR], out=TR[:, tl * tw:(tl + 1) * tw], in_=ps[:, 0:tw], func=AF.Tanh,
                 scale=0.5, bias=lc[:, 0, j, d:d + 1])
            ps, pk = psum()
            k.mm(P.matmul, [gwk, ubk], [pk], True, out=ps[:, 0:tw], lhsT=gwv[:, 2 + d, :], rhs=ubv[:, c0 + tl * tw:c0 + (tl + 1) * tw],
                 start=True, stop=True)
            k.op("act", A.activation, [pk] + LC, [KH], out=hsv[:, tl * tw:(tl + 1) * tw], in_=ps[:, 0:tw], func=AF.Tanh,
                 scale=0.5, bias=lc[:, 1, j, d:d + 1])
        k.op("act", A.activation, [KTR] + LC, [KE], out=E, in_=TR, func=AF.Exp, scale=lc[:, 3, j, d:d + 1],
             bias=lc[:, 4, j, d:d + 1])
        k.op("act", A.activation, [KTR] + LC, [KTR], out=TR, in_=TR, func=AF.Exp, scale=lc[:, 2, j, d:d + 1],
             bias=lc[:, 2, j, d:d + 1])
        k.op("dve", V.tensor_scalar, [KE], [KE], out=E, in0=E, scalar1=0.25, scalar2=0.25, op0=ALU.min, op1=ALU.subtract)
        k.op("dve", V.scalar_tensor_tensor, [KH, uk], [KH], out=hsv, in0=hsv, scalar=1.0, in1=uv, op0=ALU.add, op1=ALU.mult)

    def gate_back(d, TR, KTR, E, KE, hsv, KH, ini, ik):
        k.op("dve", V.tensor_tensor, [KE, KH], [KE], out=E, in0=E, in1=hsv, op=ALU.mult)
        if d == 0:
            k.op("dve", V.tensor_tensor_scan, [KTR, KE] + ik, [KH], out=hsv, data0=TR, data1=E, initial=ini,
                 op0=ALU.mult, op1=ALU.add)
        else:
            k.op("dve", V.tensor_tensor_scan, [KTR, KE] + ik, [KH], out=rev_ap(hsv), data0=rev_ap(TR), data1=rev_ap(E),
                 initial=ini, op0=ALU.mult, op1=ALU.add)

    def lru_fronts(j, sis, gwv, gwk):
        for si in sis:
            d, h, TR, KTR, E, KE, hsv, KH = lbufs(si)
            gate_front(j, d, gwv, gwk, ubf, "ubf", h * 1024, 512, 2, TR, KTR, E, KE, hsv, KH,
                       B2[:, h * 1024:(h + 1) * 1024], K2)

    def lru_sqrt_backs(j, sis):
        for si in sis:
            d, h, TR, KTR, E, KE, hsv, KH = lbufs(si)
            k.op("act", A.activation, [KE], [KE], out=E, in_=E, func=AF.Sqrt, scale=-1.0)
        for si in sis:
            d, h, TR, KTR, E, KE, hsv, KH = lbufs(si)
            if d == 0:
                ini, ik = (hctx[:, j, 0:1], ["hctx"]) if h == 0 else (HSb[0][:, 1023:1024], [HSK(0, 0)])
            else:
                ini, ik = (hctx[:, j, 1:2], ["hctx"]) if h == 1 else (HSb[1][:, 1024:1025], [HSK(1, 1)])
            gate_back(d, TR, KTR, E, KE, hsv, KH, ini, ik)

    tps = {}

    def lru_tail_mm(j, jj, wB, kB):
        for t in range(NT):
            ps, pk = psum()
            for kc in range(8):
                k.mm(P.matmul, [kB, ("h0", kc, t)], [pk], kc == 7, out=ps, lhsT=wB[:, kc, jj * 128:(jj + 1) * 128],
                     rhs=H0v[:, kc, tsl(t)], start=(kc == 0), stop=(kc == 7))
            tps[t] = (ps, pk)

    def lru_tail_ev(j):
        k.op("dve", V.tensor_tensor, [K5, K1], [K5], out=B5[:, 0:T], in0=B5[:, 0:T], in1=B1[:, 0:T], op=ALU.add)
        for t in range(NT):
            ps, pk = tps.pop(t)
            k.op("act", A.activation, [pk], [K4], out=B4[:, tsl(t)], in_=ps, func=AF.Tanh, scale=0.5)
            k.op("dve", V.scalar_tensor_tensor, [K4, pk], [K4], out=B4[:, tsl(t)], in0=B4[:, tsl(t)], scalar=1.0, in1=ps,
                 op0=ALU.add, op1=ALU.mult)
            k.op("dve", V.scalar_tensor_tensor, [K5, K4], [("yb", j, t)], out=Ybv[:, j, tsl(t)], in0=B5[:, tsl(t)],
                 scalar=0.5, in1=B4[:, tsl(t)], op0=ALU.mult, op1=ALU.mult)

    def ctx_unit_g(j, jj, wA, kA, gwv, gwk):
        c = j % 2
        si_ = 2 if c == 0 else 3
        S, Sb_ = scr[si_], scrb[si_]
        CK = lambda nm: ("scr", si_, "c" + nm)
        xbc = Sb_[:, 0:CTX + 3]
        ubc = Sb_[:, 264:264 + CTX]
        uc = S[:, 260:260 + CTX]
        TRc = [S[:, 516:772], S[:, 772:1028]]
        Ec = [S[:, 1028:1284], S[:, 1284:1540]]
        HSc = [S[:, 1540:1796], S[:, 1796:2052]]
        sd, sdk = build_sdiag(j)
        k.op("dve", V.memset, [], [CK("x")], ap=xbc[:, 0:2], constant=0.0)
        k.op("dve", V.memset, [], [CK("x")], ap=xbc[:, CTX + 2:CTX + 3], constant=0.0)
        yield
        ps, pk = psum()
        for kc in range(8):
            k.mm(P.matmul, [kA, ("hcT", kc)], [pk], kc == 7, out=ps[:, 0:CTX], lhsT=wA[:, kc, jj * 128:(jj + 1) * 128],
                 rhs=hcT[:, kc, :], start=(kc == 0), stop=(kc == 7))
        k.op("act", A.copy, [pk], [CK("x")], out=xbc[:, 2:2 + CTX], in_=ps[:, 0:CTX])
        yield
        ps, pk = psum()
        for i in range(4):
            k.mm(P.matmul, [CK("x"), sdk], [pk], i == 3, out=ps[:, 0:CTX], lhsT=sd[:, i, :], rhs=xbc[:, i:i + CTX],
                 start=(i == 0), stop=(i == 3))
        k.op("act", A.activation, [pk] + PR, [CK("u")], out=uc, in_=ps[:, 0:CTX], func=AF.Identity, bias=prmT[:, j, R_SB:R_SB + 1])
        yield
        k.op("dve", V.tensor_copy, [CK("u")], [CK("ub")], out=ubc, in_=uc)
        yield
        for d in range(2):
            gate_front(j, d, gwv, gwk, ubc, CK("ub"), 0, CTX, 1, TRc[d], CK("t%d" % d), Ec[d], CK("e%d" % d), HSc[d], CK("h%d" % d),
                       uc, CK("u"))
            yield
        for d in range(2):
            k.op("act", A.activation, [CK("e%d" % d)], [CK("e%d" % d)], out=Ec[d], in_=Ec[d], func=AF.Sqrt, scale=-1.0)
        yield
        for d in range(2):
            gate_back(d, TRc[d], CK("t%d" % d), Ec[d], CK("e%d" % d), HSc[d], CK("h%d" % d), 0.0, [])
            yield
        k.op("dve", V.tensor_copy, [CK("h0")], ["hctx"], out=hctx[:, j, 0:1], in_=HSc[0][:, CTX - 1:CTX])
        k.op("dve", V.tensor_copy, [CK("h1")], ["hctx"], out=hctx[:, j, 1:2], in_=HSc[1][:, 0:1])
        yield

    def run_rr(gens):
        gens = list(gens)
        while gens:
            for g in list(gens):
                try:
                    next(g)
                except StopIteration:
                    gens.remove(g)

    gw_state = {"n": 0}

    def load_gates(j, slot=None):
        i = gw_state["n"] % 2 if slot is None else slot
        gw_state["n"] = i + 1
        k.dma("pool", "gw%d" % i, out=gwf[i], in_=wri_d[:, :, j].rearrange("a d p m -> p (a d) m"), reads=[], writes=[("gw", i)])
        return gwf[i], ("gw", i)

    def conv_prep(j):
        dg = diagb[j % 2]
        for tap in range(31):
            wc = prmT[:, j, R_CW + tap:R_CW + tap + 1]
            k.op("pool", G.tensor_tensor, ["ident"] + PR, [("diag", j % 2)], out=dg[:, tap, :], in0=ident, in1=bc_ap(wc, 128),
                 op=ALU.mult)

    cps = {}

    def conv_A_mm(j, jj, wA, kA, wB, kB, t):
        psa, pka = psum()
        for kc in range(8):
            k.mm(P.matmul, [kA, ("h0", kc, t)], [pka], kc == 7, out=psa, lhsT=wA[:, kc, jj * 128:(jj + 1) * 128],
                 rhs=H0v[:, kc, tsl(t)], start=(kc == 0), stop=(kc == 7))
        psg, pkg = psum()
        for kc in range(8):
            k.mm(P.matmul, [kB, ("h0", kc, t)], [pkg], kc == 7, out=psg, lhsT=wB[:, kc, jj * 128:(jj + 1) * 128],
                 rhs=H0v[:, kc, tsl(t)], start=(kc == 0), stop=(kc == 7))
        cps[("A", t)] = (psa, pka, psg, pkg)

    def conv_A_ev(j, t):
        psa, pka, psg, pkg = cps.pop(("A", t))
        k.op("act", A.activation, [pkg], ["sg%d" % (t % 2)], out=sgt[t % 2], in_=psg, func=AF.Tanh, scale=0.5)
        k.op("dve", V.scalar_tensor_tensor, ["sg%d" % (t % 2), pka], [("upad", t)], out=upad[:, 15 + t * 512:15 + (t + 1) * 512],
             in0=sgt[t % 2], scalar=1.0, in1=psa, op0=ALU.add, op1=ALU.mult)

    def conv_B_mm(j, t):
        dg = diagb[j % 2]
        ps, pk = psum()
        rk = [("upad", tt) for tt in (t - 1, t, t + 1) if 0 <= tt < NT] + [("diag", j % 2), "upadz"]
        for tap in range(31):
            k.mm(P.matmul, rk, [pk], tap == 30, out=ps, lhsT=dg[:, tap, :],
                 rhs=upad[:, t * 512 + tap:t * 512 + tap + 512], start=(tap == 0), stop=(tap == 30))
        cps[("B", t)] = (ps, pk)

    def conv_B_ev(j, t):
        ps, pk = cps.pop(("B", t))
        k.op("act", A.activation, [pk] + PR, [("yc", j, t)], out=Ycv[:, j, tsl(t)], in_=ps, func=AF.Identity, scale=0.5,
             bias=prmT[:, j, R_CB:R_CB + 1])

    for bi in range(NB):
        k.fence()
        k.retire([("ringB", 2)])
        cT = scr[2][:, 0:8 * CTX].rearrange("p (k t) -> p k t", k=8)
        load_transposed(lambda tt: ctx_d[bi, tt * 128:(tt + 1) * 128, :], 2,
                        lambda h, tt: cT[:, h * 4:(h + 1) * 4, tt * 128:(tt + 1) * 128], lambda kk, tt: ("scr", 2))
        modulated_norm(lambda kc: cT[:, kc, :], lambda kc: ("scr", 2), CTX, 0, 4, lambda kc: hcT[:, kc, :], lambda kc: ("hcT", kc))
        k.fence()
        k.retire(ALLK("h1") + ALLK("x1") + R2K + ALLK("yb"))
        for t in range(NT):
            load_transposed(lambda tt: x_d[bi, (4 * t + tt) * 128:(4 * t + tt + 1) * 128, :], 4,
                            lambda h, tt: X0.q[h][:, :, (4 * t + tt) * 128:(4 * t + tt + 1) * 128], lambda kk, tt: X0.key(kk, t))
            wA, kA = wget()
            modulated_norm(lambda kc: X0.c(kc, tsl(t)), lambda kc: X0.key(kc, t), 512, 0, bi,
                           lambda kc: H0v[:, kc, tsl(t)], lambda kc: ("h0", kc, t))
            gws = [load_gates(2 * t + jj, slot=jj) for jj in range(2)]
            run_rr([ctx_unit_g(2 * t + jj, jj, wA, kA, gws[jj][0], gws[jj][1]) for jj in range(2)])
        k.fence()
        k.retire(ALLK("x0") + ALLK("D") + [("hcT", kc) for kc in range(8)])
        k.op("dve", V.memset, [], ["upadz"], ap=upad[:, 0:15], constant=0.0)
        k.op("dve", V.memset, [], ["upadz"], ap=upad[:, 15 + T:30 + T], constant=0.0)
        wAB, cAB = {}, {}

        def get_blocks(blk):
            if blk not in wAB:
                wAB[blk] = [wget(), None]
            return wAB[blk]

        def get_wB(blk):
            if wAB[blk][1] is None:
                wAB[blk][1] = wget()
            return wAB[blk][1]

        def get_conv(blk):
            if blk not in cAB:
                cAB[blk] = (wgetB(), wgetB())
            return cAB[blk]

        wA, kA = get_blocks(0)[0]
        lru_head(0, 0, wA, kA)
        conv_prep(0)
        for j in range(8):
            blk, jj = j // 2, j % 2
            gwv, gwk = load_gates(j)
            wB, kB = get_wB(blk)
            (cwA, ckA), (cwB, ckB) = get_conv(blk)
            lru_fronts(j, (0, 1), gwv, gwk)
            for t in range(3):
                conv_A_mm(j, jj, cwA, ckA, cwB, ckB, t)
            lru_sqrt_backs(j, (0, 1))
            for t in range(3):
                conv_A_ev(j, t)
            conv_A_mm(j, jj, cwA, ckA, cwB, ckB, 3)
            conv_A_ev(j, 3)
            if j < 7:
                conv_prep(j + 1)
            lru_fronts(j, (2, 3), gwv, gwk)
            if j < 7:
                wA2, kA2 = get_blocks((j + 1) // 2)[0]
                lru_head(j + 1, (j + 1) % 2, wA2, kA2)
            conv_B_mm(j, 0)
            conv_B_mm(j, 1)
            lru_sqrt_backs(j, (2, 3))
            conv_B_ev(j, 0)
            conv_B_ev(j, 1)
            conv_B_mm(j, 2)
            conv_B_mm(j, 3)
            lru_tail_mm(j, jj, wB, kB)
            lru_tail_ev(j)
            conv_B_ev(j, 2)
            conv_B_ev(j, 3)

        k.fence()
        RSv = lambda t: scr[0][:, tsl(t)]
        NBv = lambda t: scr[1][:, tsl(t)]
        sq2 = [scrb[2][:, 0:512], scrb[2][:, 512:1024]]
        SQ = lambda i: ("scr", 2, "q%d" % i)
        m2 = scr[3][:, 0:512]
        for t in range(NT):
            pss, pks = psum()
            psq, pkq = psum()
            for j in range(8):
                k.op("act", A.activation, [("yc", j, t)], [SQ(j % 2)], out=sq2[j % 2], in_=Ycv[:, j, tsl(t)], func=AF.Square)
                k.mm(P.matmul, [("yc", j, t), "ones"], [pks], True, out=pss, lhsT=ones_bf, rhs=Ycv[:, j, tsl(t)],
                     start=(j == 0), stop=(j == 7))
                k.mm(P.matmul, [SQ(j % 2), "ones"], [pkq], True, out=psq, lhsT=ones_bf, rhs=sq2[j % 2],
                     start=(j == 0), stop=(j == 7))
            k.op("dve", V.tensor_scalar, [pks], [("scr", 1, "nb%d" % t)], out=NBv(t), in0=pss, scalar1=1.0 / D, scalar2=None, op0=ALU.mult)
            k.op("dve", V.tensor_tensor, [("scr", 1, "nb%d" % t)], [("scr", 3, "m2")], out=m2, in0=NBv(t), in1=NBv(t), op=ALU.mult)
            k.op("dve", V.scalar_tensor_tensor, [pkq, ("scr", 3, "m2")], [("scr", 0, "rs%d" % t)], out=RSv(t), in0=psq, scalar=1.0 / D,
                 in1=m2, op0=ALU.mult, op1=ALU.subtract)
            k.op("dve", V.tensor_scalar, [("scr", 0, "rs%d" % t)], [("scr", 0, "rs%d" % t)], out=RSv(t), in0=RSv(t), scalar1=0.0,
                 scalar2=LN_EPS, op0=ALU.max, op1=ALU.add)
            k.op("act", A.activation, [("scr", 0, "rs%d" % t)], [("scr", 0, "rs%d" % t)], out=RSv(t), in_=RSv(t), func=AF.Sqrt)
            k.op("dve", V.reciprocal, [("scr", 0, "rs%d" % t)], [("scr", 0, "rs%d" % t)], out=RSv(t), in_=RSv(t))
            k.op("dve", V.scalar_tensor_tensor, [("scr", 1, "nb%d" % t), ("scr", 0, "rs%d" % t)], [("scr", 1, "nb%d" % t)],
                 out=NBv(t), in0=NBv(t), scalar=-1.0, in1=RSv(t), op0=ALU.mult, op1=ALU.mult)
        k.fence()
        k.retire(R2K + ALLK("x0"))
        tA = [scr[2][:, 1024:1536], scr[2][:, 1536:2048]]
        tB = [scr[3][:, 512:1024], scr[3][:, 1024:1536]]
        tC = [scr[4][:, 0:512], scr[4][:, 512:1024]]
        tD = [scr[4][:, 1024:1536], scr[4][:, 1536:2048]]
        it = 0
        for blk in range(4):
            wA, kA = wget()
            for jj in range(2):
                j = blk * 2 + jj
                for t in range(NT):
                    i2 = it % 2
                    it += 1
                    KA_, KB_, KC_, KD_ = ("scr", 2, "tA%d" % i2), ("scr", 3, "tB%d" % i2), ("scr", 4, "tC%d" % i2), ("scr", 4, "tD%d" % i2)
                    if SPLIT_RELOAD and SPLIT_Q0_INTERLEAVE and it <= 16:
                        tt_ = it - 1
                        if tt_ % 2 == 0:
                            stv, stk = scr[3][:, 1536:2048], ("scr", 3, "stg0")
                        else:
                            stv, stk = scr[2][:, 512:1024], ("scr", 2, "stg1")
                        reload_tile(bi, tt_, 0, stv, stk, "rst%d" % (tt_ % 2))
                    ps, pk = psum()
                    for kc in range(8):
                        k.mm(P.matmul, [kA, ("h0", kc, t)], [pk], kc == 7, out=ps, lhsT=wA[:, kc, jj * 128:(jj + 1) * 128],
                             rhs=H0v[:, kc, tsl(t)], start=(kc == 0), stop=(kc == 7))
                    k.op("act", A.activation, [pk], [KA_], out=tA[i2], in_=ps, func=AF.Silu)
                    k.op("dve", V.tensor_tensor, [("yc", j, t), ("scr", 0, "rs%d" % t)], [KB_], out=tB[i2], in0=Ycv[:, j, tsl(t)],
                         in1=RSv(t), op=ALU.mult)
                    k.op("dve", V.tensor_tensor, [KB_, ("scr", 1, "nb%d" % t)], [KB_], out=tB[i2], in0=tB[i2], in1=NBv(t), op=ALU.add)
                    k.op("act", A.activation, [KB_] + PR, [KB_], out=tB[i2], in_=tB[i2], func=AF.Identity,
                         scale=prmT[:, j, R_LG:R_LG + 1], bias=prmT[:, j, R_LB:R_LB + 1])
                    k.op("act", A.activation, [KB_], [KC_], out=tC[i2], in_=tB[i2], func=AF.Silu)
                    k.op("dve", V.tensor_tensor, [KC_, KA_], [("yc", j, t)], out=Ycv[:, j, tsl(t)], in0=tC[i2], in1=tA[i2], op=ALU.mult)
        k.fence()
        k.retire(ALLK("h0"))

        def hook_q1(step):
            stv = scr[step % 2][:, 0:512]
            reload_tile(bi, step, 1, stv, ("scr", step % 2), "xst%d" % (step % 2))

        if SPLIT_RELOAD and not SPLIT_Q0_INTERLEAVE:
            for tt_ in range(16):
                stv = scr[2 + tt_ % 2][:, 0:512]
                reload_tile(bi, tt_, 0, stv, ("scr", 2 + tt_ % 2), "rst%d" % (tt_ % 2))
        if not SPLIT_RELOAD:
            k.retire(R2K + ALLK("x0"))
            load_transposed(lambda tt: x_d[bi, tt * 128:(tt + 1) * 128, :], 16,
                            lambda h, tt: X1.q[h][:, :, tt * 128:(tt + 1) * 128], lambda kk, tt: X1.key(kk, tt // 4))
        out_proj(0, bi, X1, Ybv, "yb", mblks=range(0, 2), hook=hook_q1 if SPLIT_RELOAD else None)
        out_proj(0, bi, X1, Ybv, "yb", mblks=range(2, 4))
        k.fence()
        k.retire(ALLK("yb"))

        def h1_tile(t):
            modulated_norm(lambda kc: X1.c(kc, tsl(t)), lambda kc: X1.key(kc, t), 512, 1, bi,
                           lambda kc: H1v[:, kc, tsl(t)], lambda kc: ("h1", kc, t))

        out_proj_touter(0, bi, X1, Ycv, "yc", h1_tile)
        k.retire(ALLK("yc"))
        k.fence()
        Sb, Pb = scr[1], scr[2]
        KS, KP = ("scr", 1), ("scr", 2)
        tE = [scr[4][:, 0:512], scr[4][:, 512:1024]]
        ich = 0
        for hf in range(2):
            for blk in range(4):
                wA, kA = wget()
                for jj in range(2):
                    jl = blk * 2 + jj
                    g = (hf * 8 + jl) // 4
                    w = POOL_W[g]
                    hw_ = w // 2
                    Ub, KU = (scr[0], ("scr", 0)) if ich % 2 == 0 else (scr[3], ("scr", 3))
                    ich += 1
                    for t in range(NT):
                        ps, pk = psum()
                        for kc in range(8):
                            k.mm(P.matmul, [kA, ("h1", kc, t)], [pk], kc == 7, out=ps, lhsT=wA[:, kc, jj * 128:(jj + 1) * 128],
                                 rhs=H1v[:, kc, tsl(t)], start=(kc == 0), stop=(kc == 7))
                        k.op("act", A.copy, [pk], [KU], out=Ub[:, tsl(t)], in_=ps)
                    k.op("dve", V.memset, [], [KS], ap=Sb[:, 0:1], constant=0.0)
                    k.op("dve", V.tensor_tensor_scan, [KU, "misc"], [KS], out=Sb[:, 1:T + 1], data0=bc_ap(onecol, T), data1=Ub[:, 0:T],
                         initial=0.0, op0=ALU.mult, op1=ALU.add)
                    n_in = 64 - w + 1
                    k.op("dve", V.tensor_tensor, [KS], [KP], out=rows_ap(Pb, hw_, n_in), in0=rows_ap(Sb, w, n_in),
                         in1=rows_ap(Sb, 0, n_in), op=ALU.subtract)
                    k.op("dve", V.tensor_tensor, [KS], [KP], out=rows_ap(Pb, 0, hw_), in0=rows_ap(Sb, hw_, hw_),
                         in1=rows_ap(Sb, 0, hw_, bcast=True), op=ALU.subtract)
                    icl = wic[:, g, 0:1]
                    k.op("dve", V.tensor_tensor, [KP, "icnt"], [KP], out=rows_ap(Pb, 0, hw_), in0=rows_ap(Pb, 0, hw_),
                         in1=bass.AP(icl.tensor, icl.offset, [list(icl.ap[0]), [0, 32], [1, hw_]]), op=ALU.mult)
                    if hw_ > 1:
                        k.op("dve", V.tensor_tensor, [KS], [KP], out=rows_ap(Pb, 64 - hw_ + 1, hw_ - 1),
                             in0=rows_ap(Sb, 64, hw_ - 1, bcast=True), in1=rows_ap(Sb, 65 - w, hw_ - 1), op=ALU.subtract)
                        icr = wic[:, g, 64 - hw_ + 1:64 - hw_ + 2]
                        k.op("dve", V.tensor_tensor, [KP, "icnt"], [KP], out=rows_ap(Pb, 64 - hw_ + 1, hw_ - 1),
                             in0=rows_ap(Pb, 64 - hw_ + 1, hw_ - 1),
                             in1=bass.AP(icr.tensor, icr.offset, [list(icr.ap[0]), [0, 32], [1, hw_ - 1]]), op=ALU.mult)
                    k.op("dve", V.scalar_tensor_tensor, [KP, KU], [("D", jl, t) for t in range(NT)], out=Ycv[:, jl, :], in0=Pb[:, 0:T],
                         scalar=1.0 / w, in1=Ub[:, 0:T], op0=ALU.mult, op1=ALU.subtract)
            for gl in range(2):
                g = hf * 2 + gl
                wG, kG = wget()
                wg1, kg1 = wget()
                wg2, kg2 = wget()
                for t in range(NT):
                    pys = []
                    for mo in range(4):
                        ps, pk = psum()
                        for ki in range(4):
                            k.mm(P.matmul, [kG, ("D", gl * 4 + ki, t)], [pk], ki == 3, out=ps, lhsT=wG[:, ki, mo * 128:(mo + 1) * 128],
                                 rhs=Ycv[:, gl * 4 + ki, tsl(t)], start=(ki == 0), stop=(ki == 3))
                        pys.append((ps, pk))
                    for mo in range(4):
                        wgv, kgv = (wg1, kg1) if mo < 2 else (wg2, kg2)
                        ps, pk = psum()
                        for kc in range(8):
                            k.mm(P.matmul, [kgv, ("h1", kc, t)], [pk], kc == 7, out=ps, lhsT=wgv[:, kc, (mo % 2) * 128:(mo % 2 + 1) * 128],
                                 rhs=H1v[:, kc, tsl(t)], start=(kc == 0), stop=(kc == 7))
                        KE = ("scr", 4, "tE%d" % (mo % 2))
                        k.op("act", A.activation, [pk], [KE], out=tE[mo % 2], in_=ps, func=AF.Tanh, scale=0.5)
                        k.op("dve", V.scalar_tensor_tensor, [KE, pk], [KE], out=tE[mo % 2], in0=tE[mo % 2], scalar=1.0, in1=ps,
                             op0=ALU.add, op1=ALU.mult)
                        psy, pky = pys[mo]
                        k.op("dve", V.scalar_tensor_tensor, [pky, KE] + LC, [("D", gl * 4 + mo, t)], out=Ycv[:, gl * 4 + mo, tsl(t)], in0=psy,
                             scalar=hsc[:, (g * 4 + mo) // 8, (g * 4 + mo) % 8:(g * 4 + mo) % 8 + 1], in1=tE[mo % 2],
                             op0=ALU.mult, op1=ALU.mult)
            if hf == 0:
                out_proj(1, bi, X1, Ycv, "D")

        k.fence()
        fgc = lambda kc: prmT[:, kc, R_FG:R_FG + 1]

        def final_tile(t):
            rs = rs_t
            rstd_tile(lambda kc: X1.c(kc, tsl(t)), lambda kc: X1.key(kc, t), 512, rs)
            nrm = [scr[kc // 4][:, (kc % 4) * 512:(kc % 4 + 1) * 512] for kc in range(8)]
            NK = lambda kc: ("scr", kc // 4, "n%d" % kc)
            for kc in range(8):
                k.op("dve", V.scalar_tensor_tensor, [X1.key(kc, t), SM("rs")] + PR, [NK(kc)], out=nrm[kc], in0=X1.c(kc, tsl(t)),
                     scalar=fgc(kc), in1=rs, op0=ALU.mult, op1=ALU.mult)
            for q in range(4):
                oi = q % 2
                ost = scr[2 + oi][:, 0:D]
                OK = ("scr", 2 + oi)
                for h in range(2):
                    ps, pk = psum()
                    for qq in range(4):
                        kc = h * 4 + qq
                        k.mm(P.transpose, [NK(kc), "ident"], [pk], qq == 3, out=ps[:, qq * 128:(qq + 1) * 128],
                             in_=nrm[kc][:, q * 128:(q + 1) * 128], identity=ident)
                    k.op("act", A.copy, [pk], [OK], out=ost[:, h * 512:(h + 1) * 512], in_=ps)
                r0 = t * 512 + q * 128
                k.dma("act", "ost%d" % oi, out=out_d[bi, r0:r0 + 128, :], in_=ost, reads=[OK], writes=[])

        out_proj_touter(1, bi, X1, Ycv, "D", final_tile)

    assert wq["used"] == len(plan) and wqB["used"] == len(planB), (wq["used"], len(plan), wqB["used"], len(planB))
    for nm in ("ost0", "ost1"):
        sem = k.dsem[nm]
        nc.sync.wait_ge(sem, k.dcnt[sem.num])
        nc.scalar.wait_ge(sem, k.dcnt[sem.num])
    return nc


_NC_CACHE = {}


def _prm_rows(inp, core):
    b0 = core * NB
    rows = np.zeros((NROW, D), np.float32)
    rows[R_C:R_C + 4] = inp["c"][b0:b0 + 4]
    rows[R_CCTX] = inp["c_ctx"]
    rows[R_NG:R_NG + 2] = inp["norm_g"]
    rows[R_MODB:R_MODB + 6] = inp["mod_b"].reshape(6, D)
    rows[R_CW:R_CW + 31] = inp["ev_conv_w"][0]
    rows[R_CB] = inp["ev_conv_b"][0]
    rows[R_LG] = inp["ev_ln_g"][0]
    rows[R_LB] = inp["ev_ln_b"][0]
    rows[R_SW:R_SW + 4] = inp["ev_sconv_w"][0]
    rows[R_SB] = inp["ev_sconv_b"][0]
    rows[R_BR:R_BR + 2] = inp["ev_b_r"][0]
    rows[R_BI:R_BI + 2] = inp["ev_b_i"][0]
    rows[R_LAM:R_LAM + 2] = inp["ev_lam"][0]
    rows[R_OS:R_OS + 2] = inp["od_scale"][0].reshape(2, D)
    rows[R_FG] = inp["final_g"]
    return rows


def kernel(**inp):
    inp = {k_: np.asarray(v) for k_, v in inp.items()}
    if "nc" not in _NC_CACHE:
        _NC_CACHE["nc"] = build_program()
    nc = _NC_CACHE["nc"]
    shared = {
        "mod_w": np.ascontiguousarray(inp["mod_w"], np.float32),
        "ev_w_in": np.ascontiguousarray(inp["ev_w_in"][0], np.float32),
        "ev_w_ri": np.ascontiguousarray(np.stack([inp["ev_w_r"][0], inp["ev_w_i"][0]]), np.float32),
        "ev_w_out": np.ascontiguousarray(inp["ev_w_out"][0], np.float32),
        "od_w_in": np.ascontiguousarray(inp["od_w_in"][0], np.float32),
        "od_w_grp": np.ascontiguousarray(inp["od_w_grp"][0], np.float32),
        "od_w_out": np.ascontiguousarray(inp["od_w_out"][0], np.float32),
    }
    in_maps = []
    for c in range(8):
        m = dict(shared)
        m["x"] = np.ascontiguousarray(inp["x"][c * NB:(c + 1) * NB], np.float32)
        m["ctx"] = np.ascontiguousarray(inp["ctx"][c * NB:(c + 1) * NB], np.float32)
        m["prm"] = _prm_rows(inp, c)
        in_maps.append(m)
    res = run_bass_kernel_spmd(nc, in_maps, core_ids=list(range(8)))
    return np.concatenate([r["out"] for r in res.results], axis=0)
```

```python
import math
import numpy as np
import concourse.bass as bass
import concourse.mybir as mybir
from concourse.bass_utils import run_bass_kernel_spmd

F32 = mybir.dt.float32
BF16 = mybir.dt.bfloat16
AF = mybir.ActivationFunctionType
ALU = mybir.AluOpType

NB = 4
T = 2048
D = 1024
CTX = 256
NT = 4
RMS_EPS = 1e-6
LN_EPS = 1e-5
LN025 = math.log(0.25)
POOL_W = (2, 4, 8, 16)

R_C, R_CCTX, R_NG, R_MODB, R_CW, R_CB, R_LG, R_LB, R_SW, R_SB, R_BR, R_BI, R_LAM, R_OS, R_FG = \
    0, 4, 5, 7, 13, 44, 45, 46, 47, 51, 52, 54, 56, 58, 60
NROW = 64
WLA = 1
GATE_PREFETCH = False
SPLIT_RELOAD = True
SPLIT_Q0_INTERLEAVE = True


class Ev:
    __slots__ = ("sem", "val", "dma")

    def __init__(self, sem, val, dma):
        self.sem, self.val, self.dma = sem, val, dma


class Tracker:
    def __init__(self, nc):
        self.nc = nc
        self.engs = dict(pe=nc.tensor, act=nc.scalar, dve=nc.vector, pool=nc.gpsimd, sp=nc.sync)
        self.esem = {e: nc.alloc_semaphore("c_" + e) for e in ("pe", "act", "dve", "pool")}
        self.ecnt = dict.fromkeys(self.esem, 0)
        self.waited = {}
        self.W = {}
        self.R = {}
        self.children = {}
        self.pend_r, self.pend_w = [], []
        self.dsem = {}
        self.dcnt = {}
        self.psi = 0
        self.nwait = 0

    def _rel(self, key):
        if isinstance(key, tuple) and key[0] == "scr":
            if len(key) == 3:
                par = key[:2]
                self.children.setdefault(par, set()).add(key)
                return (key, par)
            return (key,) + tuple(self.children.get(key, ()))
        return (key,)

    def _wait(self, eng, ev):
        val = self.dcnt[ev.sem.num] if ev.dma else ev.val
        kk = (eng, ev.sem.num)
        if self.waited.get(kk, 0) >= val:
            return
        self.engs[eng].wait_ge(ev.sem, val)
        self.waited[kk] = val
        self.nwait += 1

    def deps(self, eng, reads, writes):
        for key in reads:
            for k2 in self._rel(key):
                ev = self.W.get(k2)
                if ev is not None:
                    self._wait(eng, ev)
        for key in writes:
            for k2 in self._rel(key):
                ev = self.W.get(k2)
                if ev is not None:
                    self._wait(eng, ev)
                for ev in self.R.get(k2, {}).values():
                    self._wait(eng, ev)

    def fence(self, engs=("act", "dve"), nscr=5):
        for e in engs:
            self.deps(e, [], [("scr", i) for i in range(nscr)])

    def retire(self, keys, engs=("act", "dve", "pool")):
        for e in engs:
            self.deps(e, [], keys)

    def commit(self, ev, reads, writes):
        for key in reads:
            d = self.R.setdefault(key, {})
            d[ev.sem.num] = ev
        for key in writes:
            self.W[key] = ev
            self.R[key] = {}

    def op(self, eng, fn, reads, writes, **kw):
        self.deps(eng, reads, writes)
        ins = fn(**kw)
        self.ecnt[eng] += 1
        ins.then_inc(self.esem[eng], 1)
        self.commit(Ev(self.esem[eng], self.ecnt[eng], False), reads, writes)
        return ins

    def mm(self, fn, reads, writes, signal, **kw):
        self.deps("pe", reads, writes)
        ins = fn(**kw)
        self.pend_r += list(reads)
        self.pend_w += list(writes)
        if signal:
            self.ecnt["pe"] += 1
            ins.then_inc(self.esem["pe"], 1)
            self.commit(Ev(self.esem["pe"], self.ecnt["pe"], False), self.pend_r, self.pend_w)
            self.pend_r, self.pend_w = [], []
        return ins

    def dma(self, q, semname, out, in_, reads, writes):
        if semname not in self.dsem:
            self.dsem[semname] = self.nc.alloc_semaphore("d_" + semname)
            self.dcnt[self.dsem[semname].num] = 0
        sem = self.dsem[semname]
        self.deps(q, reads, writes)
        ins = self.engs[q].dma_start(out=out, in_=in_)
        ins.then_inc(sem, 16)
        self.dcnt[sem.num] += 16
        self.commit(Ev(sem, self.dcnt[sem.num], True), reads, writes)
        return ins


def build_program():
    nc = bass.Bass("TRN2", target_bir_lowering=False)
    dt_in = lambda name, shape: nc.dram_tensor(name, shape, F32, kind="ExternalInput").ap()
    x_d = dt_in("x", [NB, T, D])
    ctx_d = dt_in("ctx", [NB, CTX, D])
    prm_d = dt_in("prm", [NROW, D])
    modw_d = dt_in("mod_w", [2, D, 3 * D])
    win0_d = dt_in("ev_w_in", [D, 5 * D])
    wri_d = dt_in("ev_w_ri", [2, 2, 8, 128, 128])
    wout0_d = dt_in("ev_w_out", [2 * D, D])
    win1_d = dt_in("od_w_in", [D, 4 * D])
    wgrp_d = dt_in("od_w_grp", [4, 512, 512])
    wout1_d = dt_in("od_w_out", [2 * D, D])
    out_d = nc.dram_tensor("out", [NB, T, D], F32, kind="ExternalOutput").ap()

    k = Tracker(nc)
    V, A, P, G = nc.vector, nc.scalar, nc.tensor, nc.gpsimd

    SCRW = 2056
    sizes = [("R1", 4 * T), ("R2", 4 * T), ("R3", 4 * T), ("R4", 4 * T), ("hcT", 4 * CTX), ("scr", 5 * SCRW), ("ubf", T // 2),
             ("ring", 4 * 1024), ("gw", 2 * 256), ("prmT", 8 * NROW), ("modT", 256), ("gsT", 128), ("lc", 128),
             ("hctx", 16), ("scT", 32), ("misc", 64), ("ident", 128), ("ones", 64), ("icnt", 4 * 64), ("wic", 4 * 64),
             ("xbpb", 1032), ("sdiag", 512)]
    off = {}
    tot = 0
    for n_, s_ in sizes:
        off[n_] = tot
        tot += s_
    arena = nc.alloc_sbuf_tensor("arena", [128, tot], F32)

    def fv(name, n, o=0):
        return arena[:, off[name] + o: off[name] + o + n]

    quadf = lambda r: fv(r, 4 * T).rearrange("p (k t) -> p k t", k=4)
    octb = lambda r: fv(r, 4 * T).bitcast(BF16).rearrange("p (k t) -> p k t", k=8)

    class Res:
        def __init__(self, ra, rb, pfx):
            self.q = [quadf(ra), quadf(rb)]
            self.pfx = pfx

        def c(self, kc, sl):
            return self.q[kc // 4][:, kc % 4, sl]

        def key(self, kc, t):
            return (self.pfx, kc, t)

    X0 = Res("R1", "R2", "x0")
    X1 = Res("R2", "R3", "x1")
    H0v, H1v = octb("R3"), octb("R1")
    Ybv, Ycv = octb("R1"), octb("R4")
    ALLK = lambda p, ks=range(8): [(p, a, t) for a in ks for t in range(NT)]
    r2 = fv("R2", 4 * T)
    upad = r2[:, 0:1040].bitcast(BF16)[:, 0:T + 30]
    diagb = [r2[:, 1040 + i * 1984:1040 + (i + 1) * 1984].bitcast(BF16).rearrange("p (k m) -> p k m", k=31) for i in range(2)]
    sgt = [r2[:, 5008:5520], r2[:, 5520:6032]]
    ringBf = [r2[:, 6032:7056].bitcast(BF16), r2[:, 7056:8080].bitcast(BF16), fv("hcT", 4 * CTX).bitcast(BF16)]
    R2K = ["upad", ("diag", 0), ("diag", 1), "sg0", "sg1", ("ringB", 0), ("ringB", 1)]
    hcT = fv("hcT", 4 * CTX).bitcast(BF16).rearrange("p (k t) -> p k t", k=8)
    scr = [fv("scr", SCRW, i * SCRW) for i in range(5)]
    scrb = [s.bitcast(BF16) for s in scr]
    ubf = fv("ubf", T // 2).bitcast(BF16)
    ringf = [fv("ring", 1024, i * 1024).bitcast(BF16) for i in range(4)]
    gwf = [fv("gw", 256, i * 256).bitcast(BF16).rearrange("p (g m) -> p g m", g=4) for i in range(2)]
    prmT = fv("prmT", 8 * NROW).rearrange("p (k r) -> p k r", k=8)
    modT = fv("modT", 240).rearrange("p (l m b) -> p l m b", l=2, m=24)
    gsT = fv("gsT", 80).rearrange("p (l k b) -> p l k b", l=2, k=8)
    lc = fv("lc", 96).rearrange("p (c j d) -> p c j d", c=6, j=8)
    hctx = fv("hctx", 16).rearrange("p (j d) -> p j d", j=8)
    scT = fv("scT", 20).bitcast(BF16).rearrange("p (k b) -> p k b", k=8)
    misc = fv("misc", 64)
    onecol = misc[:, 0:1]
    hsc = misc[:, 8:24].rearrange("p (g k) -> p g k", g=2)
    hlg = misc[:, 24:32]
    hlb = misc[:, 32:40]
    ident = fv("ident", 128)
    ones_bf = fv("ones", 64).bitcast(BF16)
    icnt = fv("icnt", 256).rearrange("p (g t) -> p g t", g=4)
    wic = fv("wic", 256).rearrange("p (g t) -> p g t", g=4)
    xbpb = fv("xbpb", 1032).bitcast(BF16)
    sdiag = fv("sdiag", 512).bitcast(BF16).rearrange("p (s k m) -> p s k m", s=2, k=4)

    psall = nc.alloc_psum_tensor("psall", [128, 8 * 512], F32)

    def psum():
        i = k.psi % 8
        k.psi += 1
        return psall[:, i * 512:(i + 1) * 512], ("ps", i)

    def bc_ap(ap, n):
        return bass.AP(ap.tensor, ap.offset, [list(ap.ap[0]), [0, n]])

    def rev_ap(ap):
        n = ap.shape[1]
        return bass.AP(ap.tensor, ap.offset + (n - 1), [list(ap.ap[0]), [-1, n]])

    def rows_ap(base, o, n, bcast=False):
        a = base[:, o:o + 1]
        return bass.AP(a.tensor, a.offset, [list(a.ap[0]), [64, 32], [0 if bcast else 1, n]])

    def wblk(w2d, r0, c0, ncol=256, nk=8):
        return w2d[r0:r0 + nk * 128, c0:c0 + ncol].rearrange("(kc p) m -> p kc m", p=128)

    plan = []
    for l in range(2):
        for blk in range(12):
            plan.append((wblk(modw_d[l], 0, blk * 256), (8, 256)))
    for bi in range(NB):
        for blk in range(4):
            plan.append((wblk(win0_d, 0, 3 * D + blk * 256), (8, 256)))
        for blk in range(4):
            plan.append((wblk(win0_d, 0, 3 * D + blk * 256), (8, 256)))
            plan.append((wblk(win0_d, 0, 4 * D + blk * 256), (8, 256)))
        for blk in range(4):
            plan.append((wblk(win0_d, 0, 2 * D + blk * 256), (8, 256)))
        for mblk in range(4):
            plan.append((wblk(wout0_d, D, mblk * 256), (8, 256)))
        for mblk in range(4):
            plan.append((wblk(wout0_d, 0, mblk * 256), (8, 256)))
        for hf in range(2):
            for blk in range(4):
                plan.append((wblk(win1_d, 0, hf * D + blk * 256), (8, 256)))
            for gl in range(2):
                g = hf * 2 + gl
                plan.append((wgrp_d[g].rearrange("(ki p) m -> p ki m", p=128), (4, 512)))
                plan.append((wblk(win1_d, 0, 2 * D + g * 512), (8, 256)))
                plan.append((wblk(win1_d, 0, 2 * D + g * 512 + 256), (8, 256)))
            for mblk in range(4):
                plan.append((wblk(wout1_d, hf * D, mblk * 256), (8, 256)))
    wq = {"used": 0, "issued": 0, "views": []}
    planB = []
    for bi in range(NB):
        for blk in range(4):
            planB.append(wblk(win0_d, 0, blk * 256))
            planB.append(wblk(win0_d, 0, D + blk * 256))
    wqB = {"used": 0, "issued": 0, "views": []}

    def wgetB():
        n = wqB["used"]
        wqB["used"] += 1
        while wqB["issued"] < min(n + 2, (n // 8 + 1) * 8):
            i = wqB["issued"]
            wqB["issued"] += 1
            view = ringBf[i % 3][:, 0:2048].rearrange("p (a b) -> p a b", a=8)
            k.dma("pool", "ringB%d" % (i % 3), out=view, in_=planB[i], reads=[], writes=[("ringB", i % 3)])
            wqB["views"].append((view, ("ringB", i % 3)))
        return wqB["views"][n]

    def wget(la=None):
        la = WLA if la is None else la
        n = wq["used"]
        wq["used"] += 1
        while wq["issued"] < min(n + la + 1, len(plan)):
            i = wq["issued"]
            wq["issued"] += 1
            dram_ap, (a_, b_) = plan[i]
            view = ringf[i % 4][:, 0:a_ * b_].rearrange("p (a b) -> p a b", a=a_)
            k.dma("pool", "ring%d" % (i % 4), out=view, in_=dram_ap, reads=[], writes=[("ring", i % 4)])
            wq["views"].append((view, ("ring", i % 4)))
        return wq["views"][n]

    k.op("pool", G.memset, [], ["ident"], ap=ident, constant=0.0)
    k.op("pool", G.affine_select, ["ident"], ["ident"], out=ident, in_=ident, pattern=[[-1, 128]],
         compare_op=ALU.not_equal, fill=1.0, base=0, channel_multiplier=1)
    k.op("pool", G.memset, [], ["ones"], ap=ones_bf, constant=1.0)
    k.op("pool", G.memset, [], ["misc"], ap=onecol, constant=1.0)
    for g, w in enumerate(POOL_W):
        k.op("pool", G.memset, [], ["icnt"], ap=icnt[:, g, :], constant=1.0 / w)
        for t in range(w // 2):
            k.op("pool", G.memset, [], ["icnt"], ap=icnt[:, g, t:t + 1], constant=1.0 / (t + w // 2))
        for t in range(64 - w // 2 + 1, 64):
            k.op("pool", G.memset, [], ["icnt"], ap=icnt[:, g, t:t + 1], constant=1.0 / (64 - t + w // 2))

    for g, w in enumerate(POOL_W):
        k.op("pool", G.memset, [], ["icnt"], ap=wic[:, g, :], constant=1.0)
        for t in range(w // 2):
            k.op("pool", G.memset, [], ["icnt"], ap=wic[:, g, t:t + 1], constant=float(w) / (t + w // 2))
        for t in range(64 - w // 2 + 1, 64):
            k.op("pool", G.memset, [], ["icnt"], ap=wic[:, g, t:t + 1], constant=float(w) / (64 - t + w // 2))

    k.op("pool", G.memset, [], ["xbpad"], ap=xbpb[:, 0:2], constant=0.0)
    k.op("pool", G.memset, [], ["xbpad"], ap=xbpb[:, 2 + T:3 + T], constant=0.0)

    k.dma("sp", "prm", out=scr[0][0:NROW, 0:D], in_=prm_d, reads=[], writes=[("scr", 0)])
    for kc in range(8):
        ps, pk = psum()
        k.mm(P.transpose, [("scr", 0), "ident"], [pk], True, out=ps[:, 0:NROW],
             in_=scr[0][0:NROW, kc * 128:(kc + 1) * 128], identity=ident[0:NROW, 0:NROW])
        k.op("act", A.copy, [pk], ["prm"], out=prmT[:, kc, :], in_=ps[:, 0:NROW])
    PR = ["prm"]

    t5a = scr[1][:, 0:40].rearrange("p (k b) -> p k b", k=8)
    k.op("act", A.activation, PR, [("scr", 1)], out=t5a, in_=prmT[:, :, 0:5], func=AF.Tanh, scale=0.5)
    k.op("dve", V.scalar_tensor_tensor, PR + [("scr", 1)], [("scr", 1)], out=t5a, in0=t5a, scalar=1.0,
         in1=prmT[:, :, 0:5], op0=ALU.add, op1=ALU.mult)
    k.op("dve", V.tensor_scalar, [("scr", 1)], ["scT"], out=scT, in0=t5a, scalar1=0.5, scalar2=None, op0=ALU.mult)

    for l in range(2):
        for blk in range(12):
            wv, wk = wget()
            for mm_ in range(2):
                m = blk * 2 + mm_
                ps, pk = psum()
                for kc in range(8):
                    k.mm(P.matmul, [wk, "scT"], [pk], kc == 7, out=ps[:, 0:5], lhsT=wv[:, kc, mm_ * 128:(mm_ + 1) * 128],
                         rhs=scT[:, kc, :], start=(kc == 0), stop=(kc == 7))
                row = R_MODB + 3 * l + m // 8
                k.op("dve", V.tensor_scalar, [pk] + PR, ["mod"], out=modT[:, l, m, :], in0=ps[:, 0:5],
                     scalar1=prmT[:, m % 8, row:row + 1], scalar2=None, op0=ALU.add)
        for kc in range(8):
            k.op("dve", V.tensor_scalar, ["mod"] + PR, ["mod"], out=gsT[:, l, kc, :], in0=modT[:, l, 8 + kc, :],
                 scalar1=1.0, scalar2=prmT[:, kc, R_NG + l:R_NG + l + 1], op0=ALU.add, op1=ALU.mult)
    MOD = ["mod"]

    lam = prmT[:, :, R_LAM:R_LAM + 2]
    s1 = scr[1]
    tv = lambda i: s1[:, 64 + 16 * i:64 + 16 * i + 16].rearrange("p (j d) -> p j d", j=8)
    S1 = [("scr", 1)]
    k.op("dve", V.tensor_scalar, PR, ["lc"], out=lc[:, 0], in0=prmT[:, :, R_BR:R_BR + 2], scalar1=0.5, scalar2=None, op0=ALU.mult)
    k.op("dve", V.tensor_scalar, PR, ["lc"], out=lc[:, 1], in0=prmT[:, :, R_BI:R_BI + 2], scalar1=0.5, scalar2=None, op0=ALU.mult)
    k.op("act", A.activation, PR, S1, out=tv(0), in_=lam, func=AF.Abs)
    k.op("act", A.activation, S1, S1, out=tv(0), in_=tv(0), func=AF.Exp, scale=-1.0)
    k.op("dve", V.tensor_scalar, S1, S1, out=tv(1), in0=tv(0), scalar1=2.0, scalar2=None, op0=ALU.add)
    k.op("dve", V.reciprocal, S1, S1, out=tv(1), in_=tv(1))
    k.op("dve", V.tensor_tensor, S1, S1, out=tv(1), in0=tv(1), in1=tv(0), op=ALU.mult)
    k.op("dve", V.tensor_tensor, S1, S1, out=tv(2), in0=tv(1), in1=tv(1), op=ALU.mult)
    k.op("dve", V.memset, [], S1, ap=tv(3), constant=1.0 / 15.0)
    for cc in (13.0, 11.0, 9.0, 7.0, 5.0, 3.0, 1.0):
        k.op("dve", V.tensor_tensor, S1, S1, out=tv(3), in0=tv(3), in1=tv(2), op=ALU.mult)
        k.op("dve", V.tensor_scalar, S1, S1, out=tv(3), in0=tv(3), scalar1=1.0 / cc, scalar2=None, op0=ALU.add)
    k.op("dve", V.tensor_tensor, S1, S1, out=tv(3), in0=tv(3), in1=tv(1), op=ALU.mult)
    k.op("dve", V.tensor_scalar, PR, S1, out=tv(4), in0=lam, scalar1=-1.0, scalar2=0.0, op0=ALU.mult, op1=ALU.max)
    k.op("dve", V.scalar_tensor_tensor, S1, S1, out=tv(4), in0=tv(3), scalar=2.0, in1=tv(4), op0=ALU.mult, op1=ALU.add)
    k.op("dve", V.tensor_scalar, S1, ["lc"], out=lc[:, 3], in0=tv(4), scalar1=-8.0, scalar2=None, op0=ALU.mult)
    k.op("dve", V.tensor_scalar, S1, ["lc"], out=lc[:, 2], in0=tv(4), scalar1=-4.0, scalar2=None, op0=ALU.mult)
    k.op("dve", V.tensor_scalar, S1, ["lc"], out=lc[:, 4], in0=tv(4), scalar1=-8.0, scalar2=LN025, op0=ALU.mult, op1=ALU.add)
    k.op("dve", V.tensor_scalar, PR, ["lc"], out=hsc, in0=prmT[:, :, R_OS:R_OS + 2].rearrange("p k g -> p g k"),
         scalar1=0.5, scalar2=None, op0=ALU.mult)
    k.op("dve", V.tensor_scalar, PR, ["lc"], out=hlg, in0=prmT[:, :, R_LG], scalar1=0.5, scalar2=None, op0=ALU.mult)
    k.op("dve", V.tensor_scalar, PR, ["lc"], out=hlb, in0=prmT[:, :, R_LB], scalar1=0.5, scalar2=None, op0=ALU.mult)
    LC = ["lc"]

    tsl = lambda t: slice(t * 512, (t + 1) * 512)

    def load_transposed(src_rows, ntok_tiles, dstq, dkey):
        for tt in range(ntok_tiles):
            si = tt % 2
            st = scr[si][:, 0:D]
            k.dma("sp", "xst%d" % si, out=st, in_=src_rows(tt), reads=[], writes=[("scr", si)])
            for h in range(2):
                ps, pk = psum()
                for q in range(4):
                    kk = h * 4 + q
                    k.mm(P.transpose, [("scr", si), "ident"], [pk], q == 3, out=ps[:, q * 128:(q + 1) * 128],
                         in_=st[:, kk * 128:(kk + 1) * 128], identity=ident)
                k.op("act", A.copy, [pk], [dkey(kk_, tt) for kk_ in range(h * 4, h * 4 + 4)],
                     out=dstq(h, tt), in_=ps.rearrange("p (q t) -> p q t", q=4))

    SM = lambda name: ("scr", 4, name)
    rs_t = scr[4][:, 0:512]
    tmp_t = [scr[4][:, 512:1024], scr[4][:, 1024:1536]]
    sqb_t = [scrb[4][:, 3072:3584], scrb[4][:, 3584:4096]]

    def rstd_tile(src, skey, n, out_rs):
        ps, pk = psum()
        for kc in range(8):
            sq = sqb_t[kc % 2][:, 0:n]
            k.op("act", A.activation, [skey(kc)], [SM("sq%d" % (kc % 2))], out=sq, in_=src(kc), func=AF.Square)
            k.mm(P.matmul, [SM("sq%d" % (kc % 2)), "ones"], [pk], True, out=ps[:, 0:n], lhsT=ones_bf, rhs=sq,
                 start=(kc == 0), stop=(kc == 7))
        k.op("act", A.activation, [pk, "misc"], [SM("rs")], out=out_rs, in_=ps[:, 0:n], func=AF.Sqrt, scale=1.0 / D, bias=epsr)
        k.op("dve", V.reciprocal, [SM("rs")], [SM("rs")], out=out_rs, in_=out_rs)

    def modulated_norm(src, skey, n, l, b, dst, dkey):
        rs = rs_t[:, 0:n]
        rstd_tile(src, skey, n, rs)
        for kc in range(8):
            tm = tmp_t[kc % 2][:, 0:n]
            k.op("dve", V.scalar_tensor_tensor, [skey(kc), SM("rs")] + MOD, [SM("tm%d" % (kc % 2))], out=tm, in0=src(kc),
                 scalar=gsT[:, l, kc, b:b + 1], in1=rs, op0=ALU.mult, op1=ALU.mult)
            k.op("act", A.activation, [SM("tm%d" % (kc % 2))] + MOD, [dkey(kc)], out=dst(kc), in_=tm, func=AF.Identity,
                 bias=modT[:, l, kc, b:b + 1])

    epsr = misc[:, 1:2]
    epsl = misc[:, 2:3]
    k.op("pool", G.memset, [], ["misc"], ap=epsr, constant=RMS_EPS)
    k.op("pool", G.memset, [], ["misc"], ap=epsl, constant=LN_EPS)
    halfc, nhalfc, q25c = misc[:, 3:4], misc[:, 4:5], misc[:, 5:6]
    k.op("pool", G.memset, [], ["misc"], ap=halfc, constant=0.5)
    k.op("pool", G.memset, [], ["misc"], ap=nhalfc, constant=-0.5)
    k.op("pool", G.memset, [], ["misc"], ap=q25c, constant=0.25)

    def out_proj(l, b, res, yv, ypfx, mblks=range(4), hook=None):
        step = 0
        for mblk in mblks:
            wv, wk = wget()
            for mm_ in range(2):
                m = mblk * 2 + mm_
                for t in range(NT):
                    if hook is not None:
                        hook(step)
                    step += 1
                    ps, pk = psum()
                    for kc in range(8):
                        k.mm(P.matmul, [wk, (ypfx, kc, t)], [pk], kc == 7, out=ps, lhsT=wv[:, kc, mm_ * 128:(mm_ + 1) * 128],
                             rhs=yv[:, kc, tsl(t)], start=(kc == 0), stop=(kc == 7))
                    k.op("dve", V.scalar_tensor_tensor, [pk, res.key(m, t)] + MOD, [res.key(m, t)], out=res.c(m, tsl(t)), in0=ps,
                         scalar=modT[:, l, 16 + m, b:b + 1], in1=res.c(m, tsl(t)), op0=ALU.mult, op1=ALU.add)

    def out_proj_touter(l, b, res, yv, ypfx, after_tile):
        blocks = [wget(), wget(), wget(), wget(0)]
        for t in range(NT):
            for m in range(8):
                wv, wk = blocks[m // 2]
                mm_ = m % 2
                ps, pk = psum()
                for kc in range(8):
                    k.mm(P.matmul, [wk, (ypfx, kc, t)], [pk], kc == 7, out=ps, lhsT=wv[:, kc, mm_ * 128:(mm_ + 1) * 128],
                         rhs=yv[:, kc, tsl(t)], start=(kc == 0), stop=(kc == 7))
                k.op("dve", V.scalar_tensor_tensor, [pk, res.key(m, t)] + MOD, [res.key(m, t)], out=res.c(m, tsl(t)), in0=ps,
                     scalar=modT[:, l, 16 + m, b:b + 1], in1=res.c(m, tsl(t)), op0=ALU.mult, op1=ALU.add)
            after_tile(t)

    def reload_tile(bi, tt, h, stv, stk, semname):
        k.dma("sp", semname, out=stv, in_=x_d[bi, tt * 128:(tt + 1) * 128, h * 512:(h + 1) * 512], reads=[], writes=[stk])
        ps, pk = psum()
        for q in range(4):
            k.mm(P.transpose, [stk, "ident"], [pk], q == 3, out=ps[:, q * 128:(q + 1) * 128],
                 in_=stv[:, q * 128:(q + 1) * 128], identity=ident)
        k.op("act", A.copy, [pk], [X1.key(h * 4 + q, tt // 4) for q in range(4)],
             out=X1.q[h][:, :, tt * 128:(tt + 1) * 128], in_=ps.rearrange("p (q t) -> p q t", q=4))

    B1, B2, B4, B5, B6 = scr[0], scr[1], scr[2], scr[3], scr[4]
    K1, K2, K4, K5, K6 = [("scr", i) for i in range(5)]

    sw_ = lambda j, i: prmT[:, j, R_SW + i:R_SW + i + 1]
    HSb = [B5, B1]
    HSK = lambda d, h: ("scr", 3 if d == 0 else 0, "h%d" % h)

    def build_sdiag(j):
        sd = sdiag[:, j % 2]
        for i in range(4):
            k.op("dve", V.tensor_scalar, ["ident"] + PR, [("sdiag", j % 2)], out=sd[:, i, :], in0=ident, scalar1=sw_(j, i),
                 scalar2=None, op0=ALU.mult)
        return sd, ("sdiag", j % 2)

    def lru_head(j, jj, wA, kA):
        sd, sdk = build_sdiag(j)
        for t in range(NT):
            ps, pk = psum()
            for kc in range(8):
                k.mm(P.matmul, [kA, ("h0", kc, t)], [pk], kc == 7, out=ps, lhsT=wA[:, kc, jj * 128:(jj + 1) * 128],
                     rhs=H0v[:, kc, tsl(t)], start=(kc == 0), stop=(kc == 7))
            k.op("act", A.copy, [pk], [("xbpb", t)], out=xbpb[:, 2 + t * 512:2 + (t + 1) * 512], in_=ps)
        for t in range(NT):
            ps, pk = psum()
            rk = [("xbpb", tt) for tt in (t - 1, t, t + 1) if 0 <= tt < NT] + [sdk, "xbpad"]
            for i in range(4):
                k.mm(P.matmul, rk, [pk], i == 3, out=ps, lhsT=sd[:, i, :], rhs=xbpb[:, t * 512 + i:t * 512 + i + 512],
                     start=(i == 0), stop=(i == 3))
            k.op("act", A.activation, [pk] + PR, [("scr", 1, "u%d" % t)], out=B2[:, tsl(t)], in_=ps, func=AF.Identity,
                 bias=prmT[:, j, R_SB:R_SB + 1])
        k.op("dve", V.tensor_copy, [K2], ["ubf"], out=ubf[:, 0:T], in_=B2[:, 0:T])

    LST = [(0, 0), (0, 1), (1, 1), (1, 0)]

    def lbufs(si):
        d, h = LST[si]
        st = si % 2
        return (d, h, B4[:, st * 1024:(st + 1) * 1024], ("scr", 2, "s%d" % st), B6[:, st * 1024:(st + 1) * 1024],
                ("scr", 4, "s%d" % st), HSb[d][:, h * 1024:(h + 1) * 1024], HSK(d, h))

    def gate_front(j, d, gwv, gwk, ubv, ubk, c0, tw, ntl, TR, KTR, E, KE, hsv, KH, uv, uk):
        for tl in range(ntl):
            ps, pk = psum()
            k.mm(P.matmul, [gwk, ubk], [pk], True, out=ps[:, 0:tw], lhsT=gwv[:, d, :], rhs=ubv[:, c0 + tl * tw:c0 + (tl + 1) * tw],
                 start=True, stop=True)
            k.op("act", A.activation, [pk] + LC, [KTR], out=TR[:, tl * tw:(tl + 1) * tw], in_=ps[:, 0:tw], func=AF.Tanh,
                 scale=0.5, bias=lc[:, 0, j, d:d + 1])
            ps, pk = psum()
            k.mm(P.matmul, [gwk, ubk], [pk], True, out=ps[:, 0:tw], lhsT=gwv[:, 2 + d, :], rhs=ubv[:, c0 + tl * tw:c0 + (tl + 1) * tw],
                 start=True, stop=True)
            k.op("act", A.activation, [pk] + LC, [KH], out=hsv[:, tl * tw:(tl + 1) * tw], in_=ps[:, 0:tw], func=AF.Tanh,
                 scale=0.5, bias=lc[:, 1, j, d:d + 1])
        k.op("act", A.activation, [KTR] + LC, [KE], out=E, in_=TR, func=AF.Exp, scale=lc[:, 3, j, d:d + 1],
             bias=lc[:, 4, j, d:d + 1])
        k.op("act", A.activation, [KTR] + LC, [KTR], out=TR, in_=TR, func=AF.Exp, scale=lc[:, 2, j, d:d + 1],
             bias=lc[:, 2, j, d:d + 1])
        k.op("dve", V.tensor_scalar, [KE], [KE], out=E, in0=E, scalar1=0.25, scalar2=0.25, op0=ALU.min, op1=ALU.subtract)
        k.op("dve", V.scalar_tensor_tensor, [KH, uk], [KH], out=hsv, in0=hsv, scalar=1.0, in1=uv, op0=ALU.add, op1=ALU.mult)

    def gate_back(d, TR, KTR, E, KE, hsv, KH, ini, ik):
        k.op("dve", V.tensor_tensor, [KE, KH], [KE], out=E, in0=E, in1=hsv, op=ALU.mult)
        if d == 0:
            k.op("dve", V.tensor_tensor_scan, [KTR, KE] + ik, [KH], out=hsv, data0=TR, data1=E, initial=ini,
                 op0=ALU.mult, op1=ALU.add)
        else:
            k.op("dve", V.tensor_tensor_scan, [KTR, KE] + ik, [KH], out=rev_ap(hsv), data0=rev_ap(TR), data1=rev_ap(E),
                 initial=ini, op0=ALU.mult, op1=ALU.add)

    def lru_fronts(j, sis, gwv, gwk):
        for si in sis:
            d, h, TR, KTR, E, KE, hsv, KH = lbufs(si)
            gate_front(j, d, gwv, gwk, ubf, "ubf", h * 1024, 512, 2, TR, KTR, E, KE, hsv, KH,
                       B2[:, h * 1024:(h + 1) * 1024], K2)

    def lru_sqrt_backs(j, sis):
        for si in sis:
            d, h, TR, KTR, E, KE, hsv, KH = lbufs(si)
            k.op("act", A.activation, [KE], [KE], out=E, in_=E, func=AF.Sqrt, scale=-1.0)
        for si in sis:
            d, h, TR, KTR, E, KE, hsv, KH = lbufs(si)
            if d == 0:
                ini, ik = (hctx[:, j, 0:1], ["hctx"]) if h == 0 else (HSb[0][:, 1023:1024], [HSK(0, 0)])
            else:
                ini, ik = (hctx[:, j, 1:2], ["hctx"]) if h == 1 else (HSb[1][:, 1024:1025], [HSK(1, 1)])
            gate_back(d, TR, KTR, E, KE, hsv, KH, ini, ik)

    def lru_tail(j, jj, wB, kB):
        k.op("dve", V.tensor_tensor, [K5, K1], [K5], out=B5[:, 0:T], in0=B5[:, 0:T], in1=B1[:, 0:T], op=ALU.add)
        for t in range(NT):
            ps, pk = psum()
            for kc in range(8):
                k.mm(P.matmul, [kB, ("h0", kc, t)], [pk], kc == 7, out=ps, lhsT=wB[:, kc, jj * 128:(jj + 1) * 128],
                     rhs=H0v[:, kc, tsl(t)], start=(kc == 0), stop=(kc == 7))
            k.op("act", A.activation, [pk], [K4], out=B4[:, tsl(t)], in_=ps, func=AF.Tanh, scale=0.5)
            k.op("dve", V.scalar_tensor_tensor, [K4, pk], [K4], out=B4[:, tsl(t)], in0=B4[:, tsl(t)], scalar=1.0, in1=ps,
                 op0=ALU.add, op1=ALU.mult)
            k.op("dve", V.scalar_tensor_tensor, [K5, K4], [("yb", j, t)], out=Ybv[:, j, tsl(t)], in0=B5[:, tsl(t)],
                 scalar=0.5, in1=B4[:, tsl(t)], op0=ALU.mult, op1=ALU.mult)

    def ctx_unit_g(j, jj, wA, kA, gwv, gwk):
        c = j % 2
        si_ = 2 if c == 0 else 3
        S, Sb_ = scr[si_], scrb[si_]
        CK = lambda nm: ("scr", si_, "c" + nm)
        xbc = Sb_[:, 0:CTX + 3]
        ubc = Sb_[:, 264:264 + CTX]
        uc = S[:, 260:260 + CTX]
        TRc = [S[:, 516:772], S[:, 772:1028]]
        Ec = [S[:, 1028:1284], S[:, 1284:1540]]
        HSc = [S[:, 1540:1796], S[:, 1796:2052]]
        sd, sdk = build_sdiag(j)
        k.op("dve", V.memset, [], [CK("x")], ap=xbc[:, 0:2], constant=0.0)
        k.op("dve", V.memset, [], [CK("x")], ap=xbc[:, CTX + 2:CTX + 3], constant=0.0)
        yield
        ps, pk = psum()
        for kc in range(8):
            k.mm(P.matmul, [kA, ("hcT", kc)], [pk], kc == 7, out=ps[:, 0:CTX], lhsT=wA[:, kc, jj * 128:(jj + 1) * 128],
                 rhs=hcT[:, kc, :], start=(kc == 0), stop=(kc == 7))
        k.op("act", A.copy, [pk], [CK("x")], out=xbc[:, 2:2 + CTX], in_=ps[:, 0:CTX])
        yield
        ps, pk = psum()
        for i in range(4):
            k.mm(P.matmul, [CK("x"), sdk], [pk], i == 3, out=ps[:, 0:CTX], lhsT=sd[:, i, :], rhs=xbc[:, i:i + CTX],
                 start=(i == 0), stop=(i == 3))
        k.op("act", A.activation, [pk] + PR, [CK("u")], out=uc, in_=ps[:, 0:CTX], func=AF.Identity, bias=prmT[:, j, R_SB:R_SB + 1])
        yield
        k.op("dve", V.tensor_copy, [CK("u")], [CK("ub")], out=ubc, in_=uc)
        yield
        for d in range(2):
            gate_front(j, d, gwv, gwk, ubc, CK("ub"), 0, CTX, 1, TRc[d], CK("t%d" % d), Ec[d], CK("e%d" % d), HSc[d], CK("h%d" % d),
                       uc, CK("u"))
            yield
        for d in range(2):
            k.op("act", A.activation, [CK("e%d" % d)], [CK("e%d" % d)], out=Ec[d], in_=Ec[d], func=AF.Sqrt, scale=-1.0)
        yield
        for d in range(2):
            gate_back(d, TRc[d], CK("t%d" % d), Ec[d], CK("e%d" % d), HSc[d], CK("h%d" % d), 0.0, [])
            yield
        k.op("dve", V.tensor_copy, [CK("h0")], ["hctx"], out=hctx[:, j, 0:1], in_=HSc[0][:, CTX - 1:CTX])
        k.op("dve", V.tensor_copy, [CK("h1")], ["hctx"], out=hctx[:, j, 1:2], in_=HSc[1][:, 0:1])
        yield

    def run_rr(gens):
        gens = list(gens)
        while gens:
            for g in list(gens):
                try:
                    next(g)
                except StopIteration:
                    gens.remove(g)

    gw_state = {"n": 0}

    def load_gates(j, slot=None):
        i = gw_state["n"] % 2 if slot is None else slot
        gw_state["n"] = i + 1
        k.dma("pool", "gw%d" % i, out=gwf[i], in_=wri_d[:, :, j].rearrange("a d p m -> p (a d) m"), reads=[], writes=[("gw", i)])
        return gwf[i], ("gw", i)

    def conv_prep(j):
        dg = diagb[j % 2]
        for tap in range(31):
            k.op("dve", V.tensor_scalar, ["ident"] + PR, [("diag", j % 2)], out=dg[:, tap, :], in0=ident,
                 scalar1=prmT[:, j, R_CW + tap:R_CW + tap + 1], scalar2=0.5, op0=ALU.mult, op1=ALU.mult)

    def conv_A(j, jj, wA, kA, wB, kB, t):
        psa, pka = psum()
        for kc in range(8):
            k.mm(P.matmul, [kA, ("h0", kc, t)], [pka], kc == 7, out=psa, lhsT=wA[:, kc, jj * 128:(jj + 1) * 128],
                 rhs=H0v[:, kc, tsl(t)], start=(kc == 0), stop=(kc == 7))
        psg, pkg = psum()
        for kc in range(8):
            k.mm(P.matmul, [kB, ("h0", kc, t)], [pkg], kc == 7, out=psg, lhsT=wB[:, kc, jj * 128:(jj + 1) * 128],
                 rhs=H0v[:, kc, tsl(t)], start=(kc == 0), stop=(kc == 7))
        k.op("act", A.activation, [pkg], ["sg%d" % (t % 2)], out=sgt[t % 2], in_=psg, func=AF.Tanh, scale=0.5)
        k.op("dve", V.scalar_tensor_tensor, ["sg%d" % (t % 2), pka], [("upad", t)], out=upad[:, 15 + t * 512:15 + (t + 1) * 512],
             in0=sgt[t % 2], scalar=1.0, in1=psa, op0=ALU.add, op1=ALU.mult)

    def conv_B(j, t):
        dg = diagb[j % 2]
        ps, pk = psum()
        rk = [("upad", tt) for tt in (t - 1, t, t + 1) if 0 <= tt < NT] + [("diag", j % 2), "upadz"]
        for tap in range(31):
            k.mm(P.matmul, rk, [pk], tap == 30, out=ps, lhsT=dg[:, tap, :],
                 rhs=upad[:, t * 512 + tap:t * 512 + tap + 512], start=(tap == 0), stop=(tap == 30))
        k.op("act", A.activation, [pk] + PR, [("yc", j, t)], out=Ycv[:, j, tsl(t)], in_=ps, func=AF.Identity,
             bias=prmT[:, j, R_CB:R_CB + 1])

    for bi in range(NB):
        k.fence()
        k.retire([("ringB", 2)])
        cT = scr[2][:, 0:8 * CTX].rearrange("p (k t) -> p k t", k=8)
        load_transposed(lambda tt: ctx_d[bi, tt * 128:(tt + 1) * 128, :], 2,
                        lambda h, tt: cT[:, h * 4:(h + 1) * 4, tt * 128:(tt + 1) * 128], lambda kk, tt: ("scr", 2))
        modulated_norm(lambda kc: cT[:, kc, :], lambda kc: ("scr", 2), CTX, 0, 4, lambda kc: hcT[:, kc, :], lambda kc: ("hcT", kc))
        k.fence()
        k.retire(ALLK("h1") + ALLK("x1") + R2K + ALLK("yb"))
        for t in range(NT):
            load_transposed(lambda tt: x_d[bi, (4 * t + tt) * 128:(4 * t + tt + 1) * 128, :], 4,
                            lambda h, tt: X0.q[h][:, :, (4 * t + tt) * 128:(4 * t + tt + 1) * 128], lambda kk, tt: X0.key(kk, t))
            wA, kA = wget()
            modulated_norm(lambda kc: X0.c(kc, tsl(t)), lambda kc: X0.key(kc, t), 512, 0, bi,
                           lambda kc: H0v[:, kc, tsl(t)], lambda kc: ("h0", kc, t))
            gws = [load_gates(2 * t + jj, slot=jj) for jj in range(2)]
            run_rr([ctx_unit_g(2 * t + jj, jj, wA, kA, gws[jj][0], gws[jj][1]) for jj in range(2)])
        k.fence()
        k.retire(ALLK("x0") + ALLK("D") + [("hcT", kc) for kc in range(8)])
        k.op("dve", V.memset, [], ["upadz"], ap=upad[:, 0:15], constant=0.0)
        k.op("dve", V.memset, [], ["upadz"], ap=upad[:, 15 + T:30 + T], constant=0.0)
        wAB, cAB = {}, {}

        def get_blocks(blk):
            if blk not in wAB:
                wAB[blk] = [wget(), None]
            return wAB[blk]

        def get_wB(blk):
            if wAB[blk][1] is None:
                wAB[blk][1] = wget()
            return wAB[blk][1]

        def get_conv(blk):
            if blk not in cAB:
                cAB[blk] = (wgetB(), wgetB())
            return cAB[blk]

        wA, kA = get_blocks(0)[0]
        lru_head(0, 0, wA, kA)
        conv_prep(0)
        for j in range(8):
            blk, jj = j // 2, j % 2
            gwv, gwk = load_gates(j)
            wB, kB = get_wB(blk)
            (cwA, ckA), (cwB, ckB) = get_conv(blk)
            lru_fronts(j, (0, 1), gwv, gwk)
            conv_A(j, jj, cwA, ckA, cwB, ckB, 0)
            conv_A(j, jj, cwA, ckA, cwB, ckB, 1)
            lru_sqrt_backs(j, (0, 1))
            conv_A(j, jj, cwA, ckA, cwB, ckB, 2)
            conv_A(j, jj, cwA, ckA, cwB, ckB, 3)
            if j < 7:
                conv_prep(j + 1)
            lru_fronts(j, (2, 3), gwv, gwk)
            conv_B(j, 0)
            conv_B(j, 1)
            if j < 7:
                wA2, kA2 = get_blocks((j + 1) // 2)[0]
                lru_head(j + 1, (j + 1) % 2, wA2, kA2)
            lru_sqrt_backs(j, (2, 3))
            lru_tail(j, jj, wB, kB)
            conv_B(j, 2)
            conv_B(j, 3)

        k.fence()
        RSv = lambda t: scr[0][:, tsl(t)]
        NBv = lambda t: scr[1][:, tsl(t)]
        sq2 = [scrb[2][:, 0:512], scrb[2][:, 512:1024]]
        SQ = lambda i: ("scr", 2, "q%d" % i)
        m2 = scr[3][:, 0:512]
        for t in range(NT):
            pss, pks = psum()
            psq, pkq = psum()
            for j in range(8):
                k.op("act", A.activation, [("yc", j, t)], [SQ(j % 2)], out=sq2[j % 2], in_=Ycv[:, j, tsl(t)], func=AF.Square)
                k.mm(P.matmul, [("yc", j, t), "ones"], [pks], True, out=pss, lhsT=ones_bf, rhs=Ycv[:, j, tsl(t)],
                     start=(j == 0), stop=(j == 7))
                k.mm(P.matmul, [SQ(j % 2), "ones"], [pkq], True, out=psq, lhsT=ones_bf, rhs=sq2[j % 2],
                     start=(j == 0), stop=(j == 7))
            k.op("dve", V.tensor_scalar, [pks], [("scr", 1, "nb%d" % t)], out=NBv(t), in0=pss, scalar1=1.0 / D, scalar2=None, op0=ALU.mult)
            k.op("dve", V.tensor_tensor, [("scr", 1, "nb%d" % t)], [("scr", 3, "m2")], out=m2, in0=NBv(t), in1=NBv(t), op=ALU.mult)
            k.op("dve", V.scalar_tensor_tensor, [pkq, ("scr", 3, "m2")], [("scr", 0, "rs%d" % t)], out=RSv(t), in0=psq, scalar=1.0 / D,
                 in1=m2, op0=ALU.mult, op1=ALU.subtract)
            k.op("dve", V.tensor_scalar, [("scr", 0, "rs%d" % t)], [("scr", 0, "rs%d" % t)], out=RSv(t), in0=RSv(t), scalar1=0.0,
                 scalar2=LN_EPS, op0=ALU.max, op1=ALU.add)
            k.op("act", A.activation, [("scr", 0, "rs%d" % t)], [("scr", 0, "rs%d" % t)], out=RSv(t), in_=RSv(t), func=AF.Sqrt)
            k.op("dve", V.reciprocal, [("scr", 0, "rs%d" % t)], [("scr", 0, "rs%d" % t)], out=RSv(t), in_=RSv(t))
            k.op("dve", V.scalar_tensor_tensor, [("scr", 1, "nb%d" % t), ("scr", 0, "rs%d" % t)], [("scr", 1, "nb%d" % t)],
                 out=NBv(t), in0=NBv(t), scalar=-1.0, in1=RSv(t), op0=ALU.mult, op1=ALU.mult)
        k.fence()
        k.retire(R2K + ALLK("x0"))
        tA = [scr[2][:, 1024:1536], scr[2][:, 1536:2048]]
        tB = [scr[3][:, 512:1024], scr[3][:, 1024:1536]]
        tC = [scr[4][:, 0:512], scr[4][:, 512:1024]]
        tD = [scr[4][:, 1024:1536], scr[4][:, 1536:2048]]
        it = 0
        for blk in range(4):
            wA, kA = wget()
            for jj in range(2):
                j = blk * 2 + jj
                for t in range(NT):
                    i2 = it % 2
                    it += 1
                    KA_, KB_, KC_, KD_ = ("scr", 2, "tA%d" % i2), ("scr", 3, "tB%d" % i2), ("scr", 4, "tC%d" % i2), ("scr", 4, "tD%d" % i2)
                    if SPLIT_RELOAD and SPLIT_Q0_INTERLEAVE and it <= 16:
                        tt_ = it - 1
                        if tt_ % 2 == 0:
                            stv, stk = scr[3][:, 1536:2048], ("scr", 3, "stg0")
                        else:
                            stv, stk = scr[2][:, 512:1024], ("scr", 2, "stg1")
                        reload_tile(bi, tt_, 0, stv, stk, "rst%d" % (tt_ % 2))
                    ps, pk = psum()
                    for kc in range(8):
                        k.mm(P.matmul, [kA, ("h0", kc, t)], [pk], kc == 7, out=ps, lhsT=wA[:, kc, jj * 128:(jj + 1) * 128],
                             rhs=H0v[:, kc, tsl(t)], start=(kc == 0), stop=(kc == 7))
                    k.op("act", A.activation, [pk], [KA_], out=tA[i2], in_=ps, func=AF.Silu)
                    k.op("dve", V.tensor_tensor, [("yc", j, t), ("scr", 0, "rs%d" % t)], [KB_], out=tB[i2], in0=Ycv[:, j, tsl(t)],
                         in1=RSv(t), op=ALU.mult)
                    k.op("dve", V.tensor_tensor, [KB_, ("scr", 1, "nb%d" % t)], [KB_], out=tB[i2], in0=tB[i2], in1=NBv(t), op=ALU.add)
                    k.op("act", A.activation, [KB_] + PR, [KB_], out=tB[i2], in_=tB[i2], func=AF.Identity,
                         scale=prmT[:, j, R_LG:R_LG + 1], bias=prmT[:, j, R_LB:R_LB + 1])
                    k.op("act", A.activation, [KB_], [KC_], out=tC[i2], in_=tB[i2], func=AF.Silu)
                    k.op("dve", V.tensor_tensor, [KC_, KA_], [("yc", j, t)], out=Ycv[:, j, tsl(t)], in0=tC[i2], in1=tA[i2], op=ALU.mult)
        k.fence()
        k.retire(ALLK("h0"))

        def hook_q1(step):
            stv = scr[step % 2][:, 0:512]
            reload_tile(bi, step, 1, stv, ("scr", step % 2), "xst%d" % (step % 2))

        if SPLIT_RELOAD and not SPLIT_Q0_INTERLEAVE:
            for tt_ in range(16):
                stv = scr[2 + tt_ % 2][:, 0:512]
                reload_tile(bi, tt_, 0, stv, ("scr", 2 + tt_ % 2), "rst%d" % (tt_ % 2))
        if not SPLIT_RELOAD:
            k.retire(R2K + ALLK("x0"))
            load_transposed(lambda tt: x_d[bi, tt * 128:(tt + 1) * 128, :], 16,
                            lambda h, tt: X1.q[h][:, :, tt * 128:(tt + 1) * 128], lambda kk, tt: X1.key(kk, tt // 4))
        out_proj(0, bi, X1, Ybv, "yb", mblks=range(0, 2), hook=hook_q1 if SPLIT_RELOAD else None)
        out_proj(0, bi, X1, Ybv, "yb", mblks=range(2, 4))
        k.fence()
        k.retire(ALLK("yb"))

        def h1_tile(t):
            modulated_norm(lambda kc: X1.c(kc, tsl(t)), lambda kc: X1.key(kc, t), 512, 1, bi,
                           lambda kc: H1v[:, kc, tsl(t)], lambda kc: ("h1", kc, t))

        out_proj_touter(0, bi, X1, Ycv, "yc", h1_tile)
        k.retire(ALLK("yc"))
        k.fence()
        Sb, Pb = scr[1], scr[2]
        KS, KP = ("scr", 1), ("scr", 2)
        tE = [scr[4][:, 0:512], scr[4][:, 512:1024]]
        ich = 0
        for hf in range(2):
            for blk in range(4):
                wA, kA = wget()
                for jj in range(2):
                    jl = blk * 2 + jj
                    g = (hf * 8 + jl) // 4
                    w = POOL_W[g]
                    hw_ = w // 2
                    Ub, KU = (scr[0], ("scr", 0)) if ich % 2 == 0 else (scr[3], ("scr", 3))
                    ich += 1
                    for t in range(NT):
                        ps, pk = psum()
                        for kc in range(8):
                            k.mm(P.matmul, [kA, ("h1", kc, t)], [pk], kc == 7, out=ps, lhsT=wA[:, kc, jj * 128:(jj + 1) * 128],
                                 rhs=H1v[:, kc, tsl(t)], start=(kc == 0), stop=(kc == 7))
                        k.op("act", A.copy, [pk], [KU], out=Ub[:, tsl(t)], in_=ps)
                    k.op("dve", V.memset, [], [KS], ap=Sb[:, 0:1], constant=0.0)
                    k.op("dve", V.tensor_tensor_scan, [KU, "misc"], [KS], out=Sb[:, 1:T + 1], data0=bc_ap(onecol, T), data1=Ub[:, 0:T],
                         initial=0.0, op0=ALU.mult, op1=ALU.add)
                    n_in = 64 - w + 1
                    k.op("dve", V.tensor_tensor, [KS], [KP], out=rows_ap(Pb, hw_, n_in), in0=rows_ap(Sb, w, n_in),
                         in1=rows_ap(Sb, 0, n_in), op=ALU.subtract)
                    k.op("dve", V.tensor_tensor, [KS], [KP], out=rows_ap(Pb, 0, hw_), in0=rows_ap(Sb, hw_, hw_),
                         in1=rows_ap(Sb, 0, hw_, bcast=True), op=ALU.subtract)
                    icl = wic[:, g, 0:1]
                    k.op("dve", V.tensor_tensor, [KP, "icnt"], [KP], out=rows_ap(Pb, 0, hw_), in0=rows_ap(Pb, 0, hw_),
                         in1=bass.AP(icl.tensor, icl.offset, [list(icl.ap[0]), [0, 32], [1, hw_]]), op=ALU.mult)
                    if hw_ > 1:
                        k.op("dve", V.tensor_tensor, [KS], [KP], out=rows_ap(Pb, 64 - hw_ + 1, hw_ - 1),
                             in0=rows_ap(Sb, 64, hw_ - 1, bcast=True), in1=rows_ap(Sb, 65 - w, hw_ - 1), op=ALU.subtract)
                        icr = wic[:, g, 64 - hw_ + 1:64 - hw_ + 2]
                        k.op("dve", V.tensor_tensor, [KP, "icnt"], [KP], out=rows_ap(Pb, 64 - hw_ + 1, hw_ - 1),
                             in0=rows_ap(Pb, 64 - hw_ + 1, hw_ - 1),
                             in1=bass.AP(icr.tensor, icr.offset, [list(icr.ap[0]), [0, 32], [1, hw_ - 1]]), op=ALU.mult)
                    k.op("dve", V.scalar_tensor_tensor, [KP, KU], [("D", jl, t) for t in range(NT)], out=Ycv[:, jl, :], in0=Pb[:, 0:T],
                         scalar=1.0 / w, in1=Ub[:, 0:T], op0=ALU.mult, op1=ALU.subtract)
            for gl in range(2):
                g = hf * 2 + gl
                wG, kG = wget()
                wg1, kg1 = wget()
                wg2, kg2 = wget()
                for t in range(NT):
                    pys = []
                    for mo in range(4):
                        ps, pk = psum()
                        for ki in range(4):
                            k.mm(P.matmul, [kG, ("D", gl * 4 + ki, t)], [pk], ki == 3, out=ps, lhsT=wG[:, ki, mo * 128:(mo + 1) * 128],
                                 rhs=Ycv[:, gl * 4 + ki, tsl(t)], start=(ki == 0), stop=(ki == 3))
                        pys.append((ps, pk))
                    for mo in range(4):
                        wgv, kgv = (wg1, kg1) if mo < 2 else (wg2, kg2)
                        ps, pk = psum()
                        for kc in range(8):
                            k.mm(P.matmul, [kgv, ("h1", kc, t)], [pk], kc == 7, out=ps, lhsT=wgv[:, kc, (mo % 2) * 128:(mo % 2 + 1) * 128],
                                 rhs=H1v[:, kc, tsl(t)], start=(kc == 0), stop=(kc == 7))
                        KE = ("scr", 4, "tE%d" % (mo % 2))
                        k.op("act", A.activation, [pk], [KE], out=tE[mo % 2], in_=ps, func=AF.Tanh, scale=0.5)
                        k.op("dve", V.scalar_tensor_tensor, [KE, pk], [KE], out=tE[mo % 2], in0=tE[mo % 2], scalar=1.0, in1=ps,
                             op0=ALU.add, op1=ALU.mult)
                        psy, pky = pys[mo]
                        k.op("dve", V.scalar_tensor_tensor, [pky, KE] + LC, [("D", gl * 4 + mo, t)], out=Ycv[:, gl * 4 + mo, tsl(t)], in0=psy,
                             scalar=hsc[:, (g * 4 + mo) // 8, (g * 4 + mo) % 8:(g * 4 + mo) % 8 + 1], in1=tE[mo % 2],
                             op0=ALU.mult, op1=ALU.mult)
            if hf == 0:
                out_proj(1, bi, X1, Ycv, "D")

        k.fence()
        fgc = lambda kc: prmT[:, kc, R_FG:R_FG + 1]

        def final_tile(t):
            rs = rs_t
            rstd_tile(lambda kc: X1.c(kc, tsl(t)), lambda kc: X1.key(kc, t), 512, rs)
            nrm = [scr[kc // 4][:, (kc % 4) * 512:(kc % 4 + 1) * 512] for kc in range(8)]
            NK = lambda kc: ("scr", kc // 4, "n%d" % kc)
            for kc in range(8):
                k.op("dve", V.scalar_tensor_tensor, [X1.key(kc, t), SM("rs")] + PR, [NK(kc)], out=nrm[kc], in0=X1.c(kc, tsl(t)),
                     scalar=fgc(kc), in1=rs, op0=ALU.mult, op1=ALU.mult)
            for q in range(4):
                oi = q % 2
                ost = scr[2 + oi][:, 0:D]
                OK = ("scr", 2 + oi)
                for h in range(2):
                    ps, pk = psum()
                    for qq in range(4):
                        kc = h * 4 + qq
                        k.mm(P.transpose, [NK(kc), "ident"], [pk], qq == 3, out=ps[:, qq * 128:(qq + 1) * 128],
                             in_=nrm[kc][:, q * 128:(q + 1) * 128], identity=ident)
                    k.op("act", A.copy, [pk], [OK], out=ost[:, h * 512:(h + 1) * 512], in_=ps)
                r0 = t * 512 + q * 128
                k.dma("act", "ost%d" % oi, out=out_d[bi, r0:r0 + 128, :], in_=ost, reads=[OK], writes=[])

        out_proj_touter(1, bi, X1, Ycv, "D", final_tile)

    assert wq["used"] == len(plan) and wqB["used"] == len(planB), (wq["used"], len(plan), wqB["used"], len(planB))
    for nm in ("ost0", "ost1"):
        sem = k.dsem[nm]
        nc.sync.wait_ge(sem, k.dcnt[sem.num])
        nc.scalar.wait_ge(sem, k.dcnt[sem.num])
    return nc


_NC_CACHE = {}


def _prm_rows(inp, core):
    b0 = core * NB
    rows = np.zeros((NROW, D), np.float32)
    rows[R_C:R_C + 4] = inp["c"][b0:b0 + 4]
    rows[R_CCTX] = inp["c_ctx"]
    rows[R_NG:R_NG + 2] = inp["norm_g"]
    rows[R_MODB:R_MODB + 6] = inp["mod_b"].reshape(6, D)
    rows[R_CW:R_CW + 31] = inp["ev_conv_w"][0]
    rows[R_CB] = inp["ev_conv_b"][0]
    rows[R_LG] = inp["ev_ln_g"][0]
    rows[R_LB] = inp["ev_ln_b"][0]
    rows[R_SW:R_SW + 4] = inp["ev_sconv_w"][0]
    rows[R_SB] = inp["ev_sconv_b"][0]
    rows[R_BR:R_BR + 2] = inp["ev_b_r"][0]
    rows[R_BI:R_BI + 2] = inp["ev_b_i"][0]
    rows[R_LAM:R_LAM + 2] = inp["ev_lam"][0]
    rows[R_OS:R_OS + 2] = inp["od_scale"][0].reshape(2, D)
    rows[R_FG] = inp["final_g"]
    return rows


def kernel(**inp):
    inp = {k_: np.asarray(v) for k_, v in inp.items()}
    if "nc" not in _NC_CACHE:
        _NC_CACHE["nc"] = build_program()
    nc = _NC_CACHE["nc"]
    shared = {
        "mod_w": np.ascontiguousarray(inp["mod_w"], np.float32),
        "ev_w_in": np.ascontiguousarray(inp["ev_w_in"][0], np.float32),
        "ev_w_ri": np.ascontiguousarray(np.stack([inp["ev_w_r"][0], inp["ev_w_i"][0]]), np.float32),
        "ev_w_out": np.ascontiguousarray(inp["ev_w_out"][0], np.float32),
        "od_w_in": np.ascontiguousarray(inp["od_w_in"][0], np.float32),
        "od_w_grp": np.ascontiguousarray(inp["od_w_grp"][0], np.float32),
        "od_w_out": np.ascontiguousarray(inp["od_w_out"][0], np.float32),
    }
    in_maps = []
    for c in range(8):
        m = dict(shared)
        m["x"] = np.ascontiguousarray(inp["x"][c * NB:(c + 1) * NB], np.float32)
        m["ctx"] = np.ascontiguousarray(inp["ctx"][c * NB:(c + 1) * NB], np.float32)
        m["prm"] = _prm_rows(inp, c)
        in_maps.append(m)
    res = run_bass_kernel_spmd(nc, in_maps, core_ids=list(range(8)))
    return np.concatenate([r["out"] for r in res.results], axis=0)
```

```python
import math
import numpy as np
import concourse.bass as bass
import concourse.mybir as mybir
from concourse.bass_utils import run_bass_kernel_spmd

F32 = mybir.dt.float32
BF16 = mybir.dt.bfloat16
AF = mybir.ActivationFunctionType
ALU = mybir.AluOpType

NB = 4
T = 2048
D = 1024
CTX = 256
NT = 4
RMS_EPS = 1e-6
LN_EPS = 1e-5
LN025 = math.log(0.25)
POOL_W = (2, 4, 8, 16)

R_C, R_CCTX, R_NG, R_MODB, R_CW, R_CB, R_LG, R_LB, R_SW, R_SB, R_BR, R_BI, R_LAM, R_OS, R_FG = \
    0, 4, 5, 7, 13, 44, 45, 46, 47, 51, 52, 54, 56, 58, 60
NROW = 64
WLA = 1
GATE_PREFETCH = False
SPLIT_RELOAD = True
SPLIT_Q0_INTERLEAVE = True


class Ev:
    __slots__ = ("sem", "val", "dma")

    def __init__(self, sem, val, dma):
        self.sem, self.val, self.dma = sem, val, dma


class Tracker:
    def __init__(self, nc):
        self.nc = nc
        self.engs = dict(pe=nc.tensor, act=nc.scalar, dve=nc.vector, pool=nc.gpsimd, sp=nc.sync)
        self.esem = {e: nc.alloc_semaphore("c_" + e) for e in ("pe", "act", "dve", "pool")}
        self.ecnt = dict.fromkeys(self.esem, 0)
        self.waited = {}
        self.W = {}
        self.R = {}
        self.children = {}
        self.pend_r, self.pend_w = [], []
        self.dsem = {}
        self.dcnt = {}
        self.psi = 0
        self.nwait = 0

    def _rel(self, key):
        if isinstance(key, tuple) and key[0] == "scr":
            if len(key) == 3:
                par = key[:2]
                self.children.setdefault(par, set()).add(key)
                return (key, par)
            return (key,) + tuple(self.children.get(key, ()))
        return (key,)

    def _wait(self, eng, ev):
        val = self.dcnt[ev.sem.num] if ev.dma else ev.val
        kk = (eng, ev.sem.num)
        if self.waited.get(kk, 0) >= val:
            return
        self.engs[eng].wait_ge(ev.sem, val)
        self.waited[kk] = val
        self.nwait += 1

    def deps(self, eng, reads, writes):
        for key in reads:
            for k2 in self._rel(key):
                ev = self.W.get(k2)
                if ev is not None:
                    self._wait(eng, ev)
        for key in writes:
            for k2 in self._rel(key):
                ev = self.W.get(k2)
                if ev is not None:
                    self._wait(eng, ev)
                for ev in self.R.get(k2, {}).values():
                    self._wait(eng, ev)

    def fence(self, engs=("act", "dve"), nscr=5):
        for e in engs:
            self.deps(e, [], [("scr", i) for i in range(nscr)])

    def retire(self, keys, engs=("act", "dve", "pool")):
        for e in engs:
            self.deps(e, [], keys)

    def commit(self, ev, reads, writes):
        for key in reads:
            d = self.R.setdefault(key, {})
            d[ev.sem.num] = ev
        for key in writes:
            self.W[key] = ev
            self.R[key] = {}

    def op(self, eng, fn, reads, writes, **kw):
        self.deps(eng, reads, writes)
        ins = fn(**kw)
        self.ecnt[eng] += 1
        ins.then_inc(self.esem[eng], 1)
        self.commit(Ev(self.esem[eng], self.ecnt[eng], False), reads, writes)
        return ins

    def mm(self, fn, reads, writes, signal, **kw):
        self.deps("pe", reads, writes)
        ins = fn(**kw)
        self.pend_r += list(reads)
        self.pend_w += list(writes)
        if signal:
            self.ecnt["pe"] += 1
            ins.then_inc(self.esem["pe"], 1)
            self.commit(Ev(self.esem["pe"], self.ecnt["pe"], False), self.pend_r, self.pend_w)
            self.pend_r, self.pend_w = [], []
        return ins

    def dma(self, q, semname, out, in_, reads, writes):
        if semname not in self.dsem:
            self.dsem[semname] = self.nc.alloc_semaphore("d_" + semname)
            self.dcnt[self.dsem[semname].num] = 0
        sem = self.dsem[semname]
        self.deps(q, reads, writes)
        ins = self.engs[q].dma_start(out=out, in_=in_)
        ins.then_inc(sem, 16)
        self.dcnt[sem.num] += 16
        self.commit(Ev(sem, self.dcnt[sem.num], True), reads, writes)
        return ins


def build_program():
    nc = bass.Bass("TRN2", target_bir_lowering=False)
    dt_in = lambda name, shape: nc.dram_tensor(name, shape, F32, kind="ExternalInput").ap()
    x_d = dt_in("x", [NB, T, D])
    ctx_d = dt_in("ctx", [NB, CTX, D])
    prm_d = dt_in("prm", [NROW, D])
    modw_d = dt_in("mod_w", [2, D, 3 * D])
    win0_d = dt_in("ev_w_in", [D, 5 * D])
    wri_d = dt_in("ev_w_ri", [2, 2, 8, 128, 128])
    wout0_d = dt_in("ev_w_out", [2 * D, D])
    win1_d = dt_in("od_w_in", [D, 4 * D])
    wgrp_d = dt_in("od_w_grp", [4, 512, 512])
    wout1_d = dt_in("od_w_out", [2 * D, D])
    out_d = nc.dram_tensor("out", [NB, T, D], F32, kind="ExternalOutput").ap()

    k = Tracker(nc)
    V, A, P, G = nc.vector, nc.scalar, nc.tensor, nc.gpsimd

    SCRW = 2056
    sizes = [("R1", 4 * T), ("R2", 4 * T), ("R3", 4 * T), ("R4", 4 * T), ("hcT", 4 * CTX), ("scr", 5 * SCRW), ("ubf", T // 2),
             ("ring", 4 * 1024), ("gw", 2 * 256), ("prmT", 8 * NROW), ("modT", 256), ("gsT", 128), ("lc", 128),
             ("hctx", 16), ("scT", 32), ("misc", 64), ("ident", 128), ("ones", 64), ("icnt", 4 * 64), ("wic", 4 * 64),
             ("xbpb", 1032), ("sdiag", 512)]
    off = {}
    tot = 0
    for n_, s_ in sizes:
        off[n_] = tot
        tot += s_
    arena = nc.alloc_sbuf_tensor("arena", [128, tot], F32)

    def fv(name, n, o=0):
        return arena[:, off[name] + o: off[name] + o + n]

    quadf = lambda r: fv(r, 4 * T).rearrange("p (k t) -> p k t", k=4)
    octb = lambda r: fv(r, 4 * T).bitcast(BF16).rearrange("p (k t) -> p k t", k=8)

    class Res:
        def __init__(self, ra, rb, pfx):
            self.q = [quadf(ra), quadf(rb)]
            self.pfx = pfx

        def c(self, kc, sl):
            return self.q[kc // 4][:, kc % 4, sl]

        def key(self, kc, t):
            return (self.pfx, kc, t)

    X0 = Res("R1", "R2", "x0")
    X1 = Res("R2", "R3", "x1")
    H0v, H1v = octb("R3"), octb("R1")
    Ybv, Ycv = octb("R1"), octb("R4")
    ALLK = lambda p, ks=range(8): [(p, a, t) for a in ks for t in range(NT)]
    r2 = fv("R2", 4 * T)
    upad = r2[:, 0:1040].bitcast(BF16)[:, 0:T + 30]
    diagb = [r2[:, 1040 + i * 1984:1040 + (i + 1) * 1984].bitcast(BF16).rearrange("p (k m) -> p k m", k=31) for i in range(2)]
    sgt = [r2[:, 5008:5520], r2[:, 5520:6032]]
    ringBf = [r2[:, 6032:7056].bitcast(BF16), r2[:, 7056:8080].bitcast(BF16), fv("hcT", 4 * CTX).bitcast(BF16)]
    R2K = ["upad", ("diag", 0), ("diag", 1), "sg0", "sg1", ("ringB", 0), ("ringB", 1)]
    hcT = fv("hcT", 4 * CTX).bitcast(BF16).rearrange("p (k t) -> p k t", k=8)
    scr = [fv("scr", SCRW, i * SCRW) for i in range(5)]
    scrb = [s.bitcast(BF16) for s in scr]
    ubf = fv("ubf", T // 2).bitcast(BF16)
    ringf = [fv("ring", 1024, i * 1024).bitcast(BF16) for i in range(4)]
    gwf = [fv("gw", 256, i * 256).bitcast(BF16).rearrange("p (g m) -> p g m", g=4) for i in range(2)]
    prmT = fv("prmT", 8 * NROW).rearrange("p (k r) -> p k r", k=8)
    modT = fv("modT", 240).rearrange("p (l m b) -> p l m b", l=2, m=24)
    gsT = fv("gsT", 80).rearrange("p (l k b) -> p l k b", l=2, k=8)
    lc = fv("lc", 96).rearrange("p (c j d) -> p c j d", c=6, j=8)
    hctx = fv("hctx", 16).rearrange("p (j d) -> p j d", j=8)
    scT = fv("scT", 20).bitcast(BF16).rearrange("p (k b) -> p k b", k=8)
    misc = fv("misc", 64)
    onecol = misc[:, 0:1]
    hsc = misc[:, 8:24].rearrange("p (g k) -> p g k", g=2)
    hlg = misc[:, 24:32]
    hlb = misc[:, 32:40]
    ident = fv("ident", 128)
    ones_bf = fv("ones", 64).bitcast(BF16)
    icnt = fv("icnt", 256).rearrange("p (g t) -> p g t", g=4)
    wic = fv("wic", 256).rearrange("p (g t) -> p g t", g=4)
    xbpb = fv("xbpb", 1032).bitcast(BF16)
    sdiag = fv("sdiag", 512).bitcast(BF16).rearrange("p (s k m) -> p s k m", s=2, k=4)

    psall = nc.alloc_psum_tensor("psall", [128, 8 * 512], F32)

    def psum():
        i = k.psi % 8
        k.psi += 1
        return psall[:, i * 512:(i + 1) * 512], ("ps", i)

    def bc_ap(ap, n):
        return bass.AP(ap.tensor, ap.offset, [list(ap.ap[0]), [0, n]])

    def rev_ap(ap):
        n = ap.shape[1]
        return bass.AP(ap.tensor, ap.offset + (n - 1), [list(ap.ap[0]), [-1, n]])

    def rows_ap(base, o, n, bcast=False):
        a = base[:, o:o + 1]
        return bass.AP(a.tensor, a.offset, [list(a.ap[0]), [64, 32], [0 if bcast else 1, n]])

    def wblk(w2d, r0, c0, ncol=256, nk=8):
        return w2d[r0:r0 + nk * 128, c0:c0 + ncol].rearrange("(kc p) m -> p kc m", p=128)

    plan = []
    for l in range(2):
        for blk in range(12):
            plan.append((wblk(modw_d[l], 0, blk * 256), (8, 256)))
    for bi in range(NB):
        for blk in range(4):
            plan.append((wblk(win0_d, 0, 3 * D + blk * 256), (8, 256)))
        for blk in range(4):
            plan.append((wblk(win0_d, 0, 3 * D + blk * 256), (8, 256)))
            plan.append((wblk(win0_d, 0, 4 * D + blk * 256), (8, 256)))
        for blk in range(4):
            plan.append((wblk(win0_d, 0, 2 * D + blk * 256), (8, 256)))
        for mblk in range(4):
            plan.append((wblk(wout0_d, D, mblk * 256), (8, 256)))
        for mblk in range(4):
            plan.append((wblk(wout0_d, 0, mblk * 256), (8, 256)))
        for hf in range(2):
            for blk in range(4):
                plan.append((wblk(win1_d, 0, hf * D + blk * 256), (8, 256)))
            for gl in range(2):
                g = hf * 2 + gl
                plan.append((wgrp_d[g].rearrange("(ki p) m -> p ki m", p=128), (4, 512)))
                plan.append((wblk(win1_d, 0, 2 * D + g * 512), (8, 256)))
                plan.append((wblk(win1_d, 0, 2 * D + g * 512 + 256), (8, 256)))
            for mblk in range(4):
                plan.append((wblk(wout1_d, hf * D, mblk * 256), (8, 256)))
    wq = {"used": 0, "issued": 0, "views": []}
    planB = []
    for bi in range(NB):
        for blk in range(4):
            planB.append(wblk(win0_d, 0, blk * 256))
            planB.append(wblk(win0_d, 0, D + blk * 256))
    wqB = {"used": 0, "issued": 0, "views": []}

    def wgetB():
        n = wqB["used"]
        wqB["used"] += 1
        while wqB["issued"] < min(n + 2, (n // 8 + 1) * 8):
            i = wqB["issued"]
            wqB["issued"] += 1
            view = ringBf[i % 3][:, 0:2048].rearrange("p (a b) -> p a b", a=8)
            k.dma("pool", "ringB%d" % (i % 3), out=view, in_=planB[i], reads=[], writes=[("ringB", i % 3)])
            wqB["views"].append((view, ("ringB", i % 3)))
        return wqB["views"][n]

    def wget(la=None):
        la = WLA if la is None else la
        n = wq["used"]
        wq["used"] += 1
        while wq["issued"] < min(n + la + 1, len(plan)):
            i = wq["issued"]
            wq["issued"] += 1
            dram_ap, (a_, b_) = plan[i]
            view = ringf[i % 4][:, 0:a_ * b_].rearrange("p (a b) -> p a b", a=a_)
            k.dma("pool", "ring%d" % (i % 4), out=view, in_=dram_ap, reads=[], writes=[("ring", i % 4)])
            wq["views"].append((view, ("ring", i % 4)))
        return wq["views"][n]

    k.op("pool", G.memset, [], ["ident"], ap=ident, constant=0.0)
    k.op("pool", G.affine_select, ["ident"], ["ident"], out=ident, in_=ident, pattern=[[-1, 128]],
         compare_op=ALU.not_equal, fill=1.0, base=0, channel_multiplier=1)
    k.op("pool", G.memset, [], ["ones"], ap=ones_bf, constant=1.0)
    k.op("pool", G.memset, [], ["misc"], ap=onecol, constant=1.0)
    for g, w in enumerate(POOL_W):
        k.op("pool", G.memset, [], ["icnt"], ap=icnt[:, g, :], constant=1.0 / w)
        for t in range(w // 2):
            k.op("pool", G.memset, [], ["icnt"], ap=icnt[:, g, t:t + 1], constant=1.0 / (t + w // 2))
        for t in range(64 - w // 2 + 1, 64):
            k.op("pool", G.memset, [], ["icnt"], ap=icnt[:, g, t:t + 1], constant=1.0 / (64 - t + w // 2))

    for g, w in enumerate(POOL_W):
        k.op("pool", G.memset, [], ["icnt"], ap=wic[:, g, :], constant=1.0)
        for t in range(w // 2):
            k.op("pool", G.memset, [], ["icnt"], ap=wic[:, g, t:t + 1], constant=float(w) / (t + w // 2))
        for t in range(64 - w // 2 + 1, 64):
            k.op("pool", G.memset, [], ["icnt"], ap=wic[:, g, t:t + 1], constant=float(w) / (64 - t + w // 2))

    k.op("pool", G.memset, [], ["xbpad"], ap=xbpb[:, 0:2], constant=0.0)
    k.op("pool", G.memset, [], ["xbpad"], ap=xbpb[:, 2 + T:3 + T], constant=0.0)

    k.dma("sp", "prm", out=scr[0][0:NROW, 0:D], in_=prm_d, reads=[], writes=[("scr", 0)])
    for kc in range(8):
        ps, pk = psum()
        k.mm(P.transpose, [("scr", 0), "ident"], [pk], True, out=ps[:, 0:NROW],
             in_=scr[0][0:NROW, kc * 128:(kc + 1) * 128], identity=ident[0:NROW, 0:NROW])
        k.op("act", A.copy, [pk], ["prm"], out=prmT[:, kc, :], in_=ps[:, 0:NROW])
    PR = ["prm"]

    t5a = scr[1][:, 0:40].rearrange("p (k b) -> p k b", k=8)
    k.op("act", A.activation, PR, [("scr", 1)], out=t5a, in_=prmT[:, :, 0:5], func=AF.Tanh, scale=0.5)
    k.op("dve", V.scalar_tensor_tensor, PR + [("scr", 1)], [("scr", 1)], out=t5a, in0=t5a, scalar=1.0,
         in1=prmT[:, :, 0:5], op0=ALU.add, op1=ALU.mult)
    k.op("dve", V.tensor_scalar, [("scr", 1)], ["scT"], out=scT, in0=t5a, scalar1=0.5, scalar2=None, op0=ALU.mult)

    for l in range(2):
        for blk in range(12):
            wv, wk = wget()
            for mm_ in range(2):
                m = blk * 2 + mm_
                ps, pk = psum()
                for kc in range(8):
                    k.mm(P.matmul, [wk, "scT"], [pk], kc == 7, out=ps[:, 0:5], lhsT=wv[:, kc, mm_ * 128:(mm_ + 1) * 128],
                         rhs=scT[:, kc, :], start=(kc == 0), stop=(kc == 7))
                row = R_MODB + 3 * l + m // 8
                k.op("dve", V.tensor_scalar, [pk] + PR, ["mod"], out=modT[:, l, m, :], in0=ps[:, 0:5],
                     scalar1=prmT[:, m % 8, row:row + 1], scalar2=None, op0=ALU.add)
        for kc in range(8):
            k.op("dve", V.tensor_scalar, ["mod"] + PR, ["mod"], out=gsT[:, l, kc, :], in0=modT[:, l, 8 + kc, :],
                 scalar1=1.0, scalar2=prmT[:, kc, R_NG + l:R_NG + l + 1], op0=ALU.add, op1=ALU.mult)
    MOD = ["mod"]

    lam = prmT[:, :, R_LAM:R_LAM + 2]
    s1 = scr[1]
    tv = lambda i: s1[:, 64 + 16 * i:64 + 16 * i + 16].rearrange("p (j d) -> p j d", j=8)
    S1 = [("scr", 1)]
    k.op("dve", V.tensor_scalar, PR, ["lc"], out=lc[:, 0], in0=prmT[:, :, R_BR:R_BR + 2], scalar1=0.5, scalar2=None, op0=ALU.mult)
    k.op("dve", V.tensor_scalar, PR, ["lc"], out=lc[:, 1], in0=prmT[:, :, R_BI:R_BI + 2], scalar1=0.5, scalar2=None, op0=ALU.mult)
    k.op("act", A.activation, PR, S1, out=tv(0), in_=lam, func=AF.Abs)
    k.op("act", A.activation, S1, S1, out=tv(0), in_=tv(0), func=AF.Exp, scale=-1.0)
    k.op("dve", V.tensor_scalar, S1, S1, out=tv(1), in0=tv(0), scalar1=2.0, scalar2=None, op0=ALU.add)
    k.op("dve", V.reciprocal, S1, S1, out=tv(1), in_=tv(1))
    k.op("dve", V.tensor_tensor, S1, S1, out=tv(1), in0=tv(1), in1=tv(0), op=ALU.mult)
    k.op("dve", V.tensor_tensor, S1, S1, out=tv(2), in0=tv(1), in1=tv(1), op=ALU.mult)
    k.op("dve", V.memset, [], S1, ap=tv(3), constant=1.0 / 15.0)
    for cc in (13.0, 11.0, 9.0, 7.0, 5.0, 3.0, 1.0):
        k.op("dve", V.tensor_tensor, S1, S1, out=tv(3), in0=tv(3), in1=tv(2), op=ALU.mult)
        k.op("dve", V.tensor_scalar, S1, S1, out=tv(3), in0=tv(3), scalar1=1.0 / cc, scalar2=None, op0=ALU.add)
    k.op("dve", V.tensor_tensor, S1, S1, out=tv(3), in0=tv(3), in1=tv(1), op=ALU.mult)
    k.op("dve", V.tensor_scalar, PR, S1, out=tv(4), in0=lam, scalar1=-1.0, scalar2=0.0, op0=ALU.mult, op1=ALU.max)
    k.op("dve", V.scalar_tensor_tensor, S1, S1, out=tv(4), in0=tv(3), scalar=2.0, in1=tv(4), op0=ALU.mult, op1=ALU.add)
    k.op("dve", V.tensor_scalar, S1, ["lc"], out=lc[:, 3], in0=tv(4), scalar1=-8.0, scalar2=None, op0=ALU.mult)
    k.op("dve", V.tensor_scalar, S1, ["lc"], out=lc[:, 2], in0=tv(4), scalar1=-4.0, scalar2=None, op0=ALU.mult)
    k.op("dve", V.tensor_scalar, S1, ["lc"], out=lc[:, 4], in0=tv(4), scalar1=-8.0, scalar2=LN025, op0=ALU.mult, op1=ALU.add)
    k.op("dve", V.tensor_scalar, PR, ["lc"], out=hsc, in0=prmT[:, :, R_OS:R_OS + 2].rearrange("p k g -> p g k"),
         scalar1=0.5, scalar2=None, op0=ALU.mult)
    k.op("dve", V.tensor_scalar, PR, ["lc"], out=hlg, in0=prmT[:, :, R_LG], scalar1=0.5, scalar2=None, op0=ALU.mult)
    k.op("dve", V.tensor_scalar, PR, ["lc"], out=hlb, in0=prmT[:, :, R_LB], scalar1=0.5, scalar2=None, op0=ALU.mult)
    LC = ["lc"]

    tsl = lambda t: slice(t * 512, (t + 1) * 512)

    def load_transposed(src_rows, ntok_tiles, dstq, dkey):
        for tt in range(ntok_tiles):
            si = tt % 2
            st = scr[si][:, 0:D]
            k.dma("sp", "xst%d" % si, out=st, in_=src_rows(tt), reads=[], writes=[("scr", si)])
            for h in range(2):
                ps, pk = psum()
                for q in range(4):
                    kk = h * 4 + q
                    k.mm(P.transpose, [("scr", si), "ident"], [pk], q == 3, out=ps[:, q * 128:(q + 1) * 128],
                         in_=st[:, kk * 128:(kk + 1) * 128], identity=ident)
                k.op("act", A.copy, [pk], [dkey(kk_, tt) for kk_ in range(h * 4, h * 4 + 4)],
                     out=dstq(h, tt), in_=ps.rearrange("p (q t) -> p q t", q=4))

    SM = lambda name: ("scr", 4, name)
    rs_t = scr[4][:, 0:512]
    tmp_t = [scr[4][:, 512:1024], scr[4][:, 1024:1536]]
    sqb_t = [scrb[4][:, 3072:3584], scrb[4][:, 3584:4096]]

    def rstd_tile(src, skey, n, out_rs):
        ps, pk = psum()
        for kc in range(8):
            sq = sqb_t[kc % 2][:, 0:n]
            k.op("act", A.activation, [skey(kc)], [SM("sq%d" % (kc % 2))], out=sq, in_=src(kc), func=AF.Square)
            k.mm(P.matmul, [SM("sq%d" % (kc % 2)), "ones"], [pk], True, out=ps[:, 0:n], lhsT=ones_bf, rhs=sq,
                 start=(kc == 0), stop=(kc == 7))
        k.op("act", A.activation, [pk, "misc"], [SM("rs")], out=out_rs, in_=ps[:, 0:n], func=AF.Sqrt, scale=1.0 / D, bias=epsr)
        k.op("dve", V.reciprocal, [SM("rs")], [SM("rs")], out=out_rs, in_=out_rs)

    def modulated_norm(src, skey, n, l, b, dst, dkey):
        rs = rs_t[:, 0:n]
        rstd_tile(src, skey, n, rs)
        for kc in range(8):
            tm = tmp_t[kc % 2][:, 0:n]
            k.op("dve", V.scalar_tensor_tensor, [skey(kc), SM("rs")] + MOD, [SM("tm%d" % (kc % 2))], out=tm, in0=src(kc),
                 scalar=gsT[:, l, kc, b:b + 1], in1=rs, op0=ALU.mult, op1=ALU.mult)
            k.op("act", A.activation, [SM("tm%d" % (kc % 2))] + MOD, [dkey(kc)], out=dst(kc), in_=tm, func=AF.Identity,
                 bias=modT[:, l, kc, b:b + 1])

    epsr = misc[:, 1:2]
    epsl = misc[:, 2:3]
    k.op("pool", G.memset, [], ["misc"], ap=epsr, constant=RMS_EPS)
    k.op("pool", G.memset, [], ["misc"], ap=epsl, constant=LN_EPS)
    halfc, nhalfc, q25c = misc[:, 3:4], misc[:, 4:5], misc[:, 5:6]
    k.op("pool", G.memset, [], ["misc"], ap=halfc, constant=0.5)
    k.op("pool", G.memset, [], ["misc"], ap=nhalfc, constant=-0.5)
    k.op("pool", G.memset, [], ["misc"], ap=q25c, constant=0.25)

    def out_proj(l, b, res, yv, ypfx, mblks=range(4), hook=None):
        step = 0
        for mblk in mblks:
            wv, wk = wget()
            for mm_ in range(2):
                m = mblk * 2 + mm_
                for t in range(NT):
                    if hook is not None:
                        hook(step)
                    step += 1
                    ps, pk = psum()
                    for kc in range(8):
                        k.mm(P.matmul, [wk, (ypfx, kc, t)], [pk], kc == 7, out=ps, lhsT=wv[:, kc, mm_ * 128:(mm_ + 1) * 128],
                             rhs=yv[:, kc, tsl(t)], start=(kc == 0), stop=(kc == 7))
                    k.op("dve", V.scalar_tensor_tensor, [pk, res.key(m, t)] + MOD, [res.key(m, t)], out=res.c(m, tsl(t)), in0=ps,
                         scalar=modT[:, l, 16 + m, b:b + 1], in1=res.c(m, tsl(t)), op0=ALU.mult, op1=ALU.add)

    def out_proj_touter(l, b, res, yv, ypfx, after_tile):
        blocks = [wget(), wget(), wget(), wget(0)]
        for t in range(NT):
            for m in range(8):
                wv, wk = blocks[m // 2]
                mm_ = m % 2
                ps, pk = psum()
                for kc in range(8):
                    k.mm(P.matmul, [wk, (ypfx, kc, t)], [pk], kc == 7, out=ps, lhsT=wv[:, kc, mm_ * 128:(mm_ + 1) * 128],
                         rhs=yv[:, kc, tsl(t)], start=(kc == 0), stop=(kc == 7))
                k.op("dve", V.scalar_tensor_tensor, [pk, res.key(m, t)] + MOD, [res.key(m, t)], out=res.c(m, tsl(t)), in0=ps,
                     scalar=modT[:, l, 16 + m, b:b + 1], in1=res.c(m, tsl(t)), op0=ALU.mult, op1=ALU.add)
            after_tile(t)

    def reload_tile(bi, tt, h, stv, stk, semname):
        k.dma("sp", semname, out=stv, in_=x_d[bi, tt * 128:(tt + 1) * 128, h * 512:(h + 1) * 512], reads=[], writes=[stk])
        ps, pk = psum()
        for q in range(4):
            k.mm(P.transpose, [stk, "ident"], [pk], q == 3, out=ps[:, q * 128:(q + 1) * 128],
                 in_=stv[:, q * 128:(q + 1) * 128], identity=ident)
        k.op("act", A.copy, [pk], [X1.key(h * 4 + q, tt // 4) for q in range(4)],
             out=X1.q[h][:, :, tt * 128:(tt + 1) * 128], in_=ps.rearrange("p (q t) -> p q t", q=4))

    B1, B2, B4, B5, B6 = scr[0], scr[1], scr[2], scr[3], scr[4]
    K1, K2, K4, K5, K6 = [("scr", i) for i in range(5)]

    sw_ = lambda j, i: prmT[:, j, R_SW + i:R_SW + i + 1]
    HSb = [B5, B1]
    HSK = lambda d, h: ("scr", 3 if d == 0 else 0, "h%d" % h)

    def build_sdiag(j):
        sd = sdiag[:, j % 2]
        for i in range(4):
            k.op("dve", V.tensor_scalar, ["ident"] + PR, [("sdiag", j % 2)], out=sd[:, i, :], in0=ident, scalar1=sw_(j, i),
                 scalar2=None, op0=ALU.mult)
        return sd, ("sdiag", j % 2)

    def lru_head(j, jj, wA, kA):
        sd, sdk = build_sdiag(j)
        for t in range(NT):
            ps, pk = psum()
            for kc in range(8):
                k.mm(P.matmul, [kA, ("h0", kc, t)], [pk], kc == 7, out=ps, lhsT=wA[:, kc, jj * 128:(jj + 1) * 128],
                     rhs=H0v[:, kc, tsl(t)], start=(kc == 0), stop=(kc == 7))
            k.op("act", A.copy, [pk], [("xbpb", t)], out=xbpb[:, 2 + t * 512:2 + (t + 1) * 512], in_=ps)
        for t in range(NT):
            ps, pk = psum()
            rk = [("xbpb", tt) for tt in (t - 1, t, t + 1) if 0 <= tt < NT] + [sdk, "xbpad"]
            for i in range(4):
                k.mm(P.matmul, rk, [pk], i == 3, out=ps, lhsT=sd[:, i, :], rhs=xbpb[:, t * 512 + i:t * 512 + i + 512],
                     start=(i == 0), stop=(i == 3))
            k.op("act", A.activation, [pk] + PR, [("scr", 1, "u%d" % t)], out=B2[:, tsl(t)], in_=ps, func=AF.Identity,
                 bias=prmT[:, j, R_SB:R_SB + 1])
        k.op("dve", V.tensor_copy, [K2], ["ubf"], out=ubf[:, 0:T], in_=B2[:, 0:T])

    LST = [(0, 0), (0, 1), (1, 1), (1, 0)]

    def lbufs(si):
        d, h = LST[si]
        st = si % 2
        return (d, h, B4[:, st * 1024:(st + 1) * 1024], ("scr", 2, "s%d" % st), B6[:, st * 1024:(st + 1) * 1024],
                ("scr", 4, "s%d" % st), HSb[d][:, h * 1024:(h + 1) * 1024], HSK(d, h))

    def gate_front(j, d, gwv, gwk, ubv, ubk, c0, tw, ntl, TR, KTR, E, KE, hsv, KH, uv, uk):
        for tl in range(ntl):
            ps, pk = psum()
            k.mm(P.matmul, [gwk, ubk], [pk], True, out=ps[:, 0:tw], lhsT=gwv[:, d, :], rhs=ubv[:, c0 + tl * tw:c0 + (tl + 1) * tw],
                 start=True, stop=True)
            k.op("act", A.activation, [pk] + LC, [KTR], out=TR[:, tl * tw:(tl + 1) * tw], in_=ps[:, 0:tw], func=AF.Tanh,
                 scale=0.5, bias=lc[:, 0, j, d:d + 1])
            ps, pk = psum()
            k.mm(P.matmul, [gwk, ubk], [pk], True, out=ps[:, 0:tw], lhsT=gwv[:, 2 + d, :], rhs=ubv[:, c0 + tl * tw:c0 + (tl + 1) * tw],
                 start=True, stop=True)
            k.op("act", A.activation, [pk] + LC, [KH], out=hsv[:, tl * tw:(tl + 1) * tw], in_=ps[:, 0:tw], func=AF.Tanh,
                 scale=0.5, bias=lc[:, 1, j, d:d + 1])
        k.op("act", A.activation, [KTR] + LC, [KE], out=E, in_=TR, func=AF.Exp, scale=lc[:, 3, j, d:d + 1],
             bias=lc[:, 4, j, d:d + 1])
        k.op("act", A.activation, [KTR] + LC, [KTR], out=TR, in_=TR, func=AF.Exp, scale=lc[:, 2, j, d:d + 1],
             bias=lc[:, 2, j, d:d + 1])
        k.op("dve", V.tensor_scalar, [KE], [KE], out=E, in0=E, scalar1=0.25, scalar2=0.25, op0=ALU.min, op1=ALU.subtract)
        k.op("dve", V.scalar_tensor_tensor, [KH, uk], [KH], out=hsv, in0=hsv, scalar=1.0, in1=uv, op0=ALU.add, op1=ALU.mult)

    def gate_back(d, TR, KTR, E, KE, hsv, KH, ini, ik):
        k.op("dve", V.tensor_tensor, [KE, KH], [KE], out=E, in0=E, in1=hsv, op=ALU.mult)
        if d == 0:
            k.op("dve", V.tensor_tensor_scan, [KTR, KE] + ik, [KH], out=hsv, data0=TR, data1=E, initial=ini,
                 op0=ALU.mult, op1=ALU.add)
        else:
            k.op("dve", V.tensor_tensor_scan, [KTR, KE] + ik, [KH], out=rev_ap(hsv), data0=rev_ap(TR), data1=rev_ap(E),
                 initial=ini, op0=ALU.mult, op1=ALU.add)

    def lru_fronts(j, sis, gwv, gwk):
        for si in sis:
            d, h, TR, KTR, E, KE, hsv, KH = lbufs(si)
            gate_front(j, d, gwv, gwk, ubf, "ubf", h * 1024, 512, 2, TR, KTR, E, KE, hsv, KH,
                       B2[:, h * 1024:(h + 1) * 1024], K2)

    def lru_sqrt_backs(j, sis):
        for si in sis:
            d, h, TR, KTR, E, KE, hsv, KH = lbufs(si)
            k.op("act", A.activation, [KE], [KE], out=E, in_=E, func=AF.Sqrt, scale=-1.0)
        for si in sis:
            d, h, TR, KTR, E, KE, hsv, KH = lbufs(si)
            if d == 0:
                ini, ik = (hctx[:, j, 0:1], ["hctx"]) if h == 0 else (HSb[0][:, 1023:1024], [HSK(0, 0)])
            else:
                ini, ik = (hctx[:, j, 1:2], ["hctx"]) if h == 1 else (HSb[1][:, 1024:1025], [HSK(1, 1)])
            gate_back(d, TR, KTR, E, KE, hsv, KH, ini, ik)

    def lru_tail(j, jj, wB, kB):
        k.op("dve", V.tensor_tensor, [K5, K1], [K5], out=B5[:, 0:T], in0=B5[:, 0:T], in1=B1[:, 0:T], op=ALU.add)
        for t in range(NT):
            ps, pk = psum()
            for kc in range(8):
                k.mm(P.matmul, [kB, ("h0", kc, t)], [pk], kc == 7, out=ps, lhsT=wB[:, kc, jj * 128:(jj + 1) * 128],
                     rhs=H0v[:, kc, tsl(t)], start=(kc == 0), stop=(kc == 7))
            k.op("act", A.activation, [pk], [K4], out=B4[:, tsl(t)], in_=ps, func=AF.Tanh, scale=0.5)
            k.op("dve", V.scalar_tensor_tensor, [K4, pk], [K4], out=B4[:, tsl(t)], in0=B4[:, tsl(t)], scalar=1.0, in1=ps,
                 op0=ALU.add, op1=ALU.mult)
            k.op("dve", V.scalar_tensor_tensor, [K5, K4], [("yb", j, t)], out=Ybv[:, j, tsl(t)], in0=B5[:, tsl(t)],
                 scalar=0.5, in1=B4[:, tsl(t)], op0=ALU.mult, op1=ALU.mult)

    def ctx_unit_g(j, jj, wA, kA, gwv, gwk):
        c = j % 2
        si_ = 2 if c == 0 else 3
        S, Sb_ = scr[si_], scrb[si_]
        CK = lambda nm: ("scr", si_, "c" + nm)
        xbc = Sb_[:, 0:CTX + 3]
        ubc = Sb_[:, 264:264 + CTX]
        uc = S[:, 260:260 + CTX]
        TRc = [S[:, 516:772], S[:, 772:1028]]
        Ec = [S[:, 1028:1284], S[:, 1284:1540]]
        HSc = [S[:, 1540:1796], S[:, 1796:2052]]
        sd, sdk = build_sdiag(j)
        k.op("dve", V.memset, [], [CK("x")], ap=xbc[:, 0:2], constant=0.0)
        k.op("dve", V.memset, [], [CK("x")], ap=xbc[:, CTX + 2:CTX + 3], constant=0.0)
        yield
        ps, pk = psum()
        for kc in range(8):
            k.mm(P.matmul, [kA, ("hcT", kc)], [pk], kc == 7, out=ps[:, 0:CTX], lhsT=wA[:, kc, jj * 128:(jj + 1) * 128],
                 rhs=hcT[:, kc, :], start=(kc == 0), stop=(kc == 7))
        k.op("act", A.copy, [pk], [CK("x")], out=xbc[:, 2:2 + CTX], in_=ps[:, 0:CTX])
        yield
        ps, pk = psum()
        for i in range(4):
            k.mm(P.matmul, [CK("x"), sdk], [pk], i == 3, out=ps[:, 0:CTX], lhsT=sd[:, i, :], rhs=xbc[:, i:i + CTX],
                 start=(i == 0), stop=(i == 3))
        k.op("act", A.activation, [pk] + PR, [CK("u")], out=uc, in_=ps[:, 0:CTX], func=AF.Identity, bias=prmT[:, j, R_SB:R_SB + 1])
        yield
        k.op("dve", V.tensor_copy, [CK("u")], [CK("ub")], out=ubc, in_=uc)
        yield
        for d in range(2):
            gate_front(j, d, gwv, gwk, ubc, CK("ub"), 0, CTX, 1, TRc[d], CK("t%d" % d), Ec[d], CK("e%d" % d), HSc[d], CK("h%d" % d),
                       uc, CK("u"))
            yield
        for d in range(2):
            k.op("act", A.activation, [CK("e%d" % d)], [CK("e%d" % d)], out=Ec[d], in_=Ec[d], func=AF.Sqrt, scale=-1.0)
        yield
        for d in range(2):
            gate_back(d, TRc[d], CK("t%d" % d), Ec[d], CK("e%d" % d), HSc[d], CK("h%d" % d), 0.0, [])
            yield
        k.op("dve", V.tensor_copy, [CK("h0")], ["hctx"], out=hctx[:, j, 0:1], in_=HSc[0][:, CTX - 1:CTX])
        k.op("dve", V.tensor_copy, [CK("h1")], ["hctx"], out=hctx[:, j, 1:2], in_=HSc[1][:, 0:1])
        yield

    def run_rr(gens):
        gens = list(gens)
        while gens:
            for g in list(gens):
                try:
                    next(g)
                except StopIteration:
                    gens.remove(g)

    gw_state = {"n": 0}

    def load_gates(j, slot=None):
        i = gw_state["n"] % 2 if slot is None else slot
        gw_state["n"] = i + 1
        k.dma("pool", "gw%d" % i, out=gwf[i], in_=wri_d[:, :, j].rearrange("a d p m -> p (a d) m"), reads=[], writes=[("gw", i)])
        return gwf[i], ("gw", i)

    def conv_prep(j):
        dg = diagb[j % 2]
        for tap in range(31):
            wc = prmT[:, j, R_CW + tap:R_CW + tap + 1]
            k.op("pool", G.tensor_tensor, ["ident"] + PR, [("diag", j % 2)], out=dg[:, tap, :], in0=ident, in1=bc_ap(wc, 128),
                 op=ALU.mult)

    def conv_A(j, jj, wA, kA, wB, kB, t):
        psa, pka = psum()
        for kc in range(8):
            k.mm(P.matmul, [kA, ("h0", kc, t)], [pka], kc == 7, out=psa, lhsT=wA[:, kc, jj * 128:(jj + 1) * 128],
                 rhs=H0v[:, kc, tsl(t)], start=(kc == 0), stop=(kc == 7))
        psg, pkg = psum()
        for kc in range(8):
            k.mm(P.matmul, [kB, ("h0", kc, t)], [pkg], kc == 7, out=psg, lhsT=wB[:, kc, jj * 128:(jj + 1) * 128],
                 rhs=H0v[:, kc, tsl(t)], start=(kc == 0), stop=(kc == 7))
        k.op("act", A.activation, [pkg], ["sg%d" % (t % 2)], out=sgt[t % 2], in_=psg, func=AF.Tanh, scale=0.5)
        k.op("dve", V.scalar_tensor_tensor, ["sg%d" % (t % 2), pka], [("upad", t)], out=upad[:, 15 + t * 512:15 + (t + 1) * 512],
             in0=sgt[t % 2], scalar=1.0, in1=psa, op0=ALU.add, op1=ALU.mult)

    def conv_B(j, t):
        dg = diagb[j % 2]
        ps, pk = psum()
        rk = [("upad", tt) for tt in (t - 1, t, t + 1) if 0 <= tt < NT] + [("diag", j % 2), "upadz"]
        for tap in range(31):
            k.mm(P.matmul, rk, [pk], tap == 30, out=ps, lhsT=dg[:, tap, :],
                 rhs=upad[:, t * 512 + tap:t * 512 + tap + 512], start=(tap == 0), stop=(tap == 30))
        k.op("act", A.activation, [pk] + PR, [("yc", j, t)], out=Ycv[:, j, tsl(t)], in_=ps, func=AF.Identity, scale=0.5,
             bias=prmT[:, j, R_CB:R_CB + 1])

    for bi in range(NB):
        k.fence()
        k.retire([("ringB", 2)])
        cT = scr[2][:, 0:8 * CTX].rearrange("p (k t) -> p k t", k=8)
        load_transposed(lambda tt: ctx_d[bi, tt * 128:(tt + 1) * 128, :], 2,
                        lambda h, tt: cT[:, h * 4:(h + 1) * 4, tt * 128:(tt + 1) * 128], lambda kk, tt: ("scr", 2))
        modulated_norm(lambda kc: cT[:, kc, :], lambda kc: ("scr", 2), CTX, 0, 4, lambda kc: hcT[:, kc, :], lambda kc: ("hcT", kc))
        k.fence()
        k.retire(ALLK("h1") + ALLK("x1") + R2K + ALLK("yb"))
        for t in range(NT):
            load_transposed(lambda tt: x_d[bi, (4 * t + tt) * 128:(4 * t + tt + 1) * 128, :], 4,
                            lambda h, tt: X0.q[h][:, :, (4 * t + tt) * 128:(4 * t + tt + 1) * 128], lambda kk, tt: X0.key(kk, t))
            wA, kA = wget()
            modulated_norm(lambda kc: X0.c(kc, tsl(t)), lambda kc: X0.key(kc, t), 512, 0, bi,
                           lambda kc: H0v[:, kc, tsl(t)], lambda kc: ("h0", kc, t))
            gws = [load_gates(2 * t + jj, slot=jj) for jj in range(2)]
            run_rr([ctx_unit_g(2 * t + jj, jj, wA, kA, gws[jj][0], gws[jj][1]) for jj in range(2)])
        k.fence()
        k.retire(ALLK("x0") + ALLK("D") + [("hcT", kc) for kc in range(8)])
        k.op("dve", V.memset, [], ["upadz"], ap=upad[:, 0:15], constant=0.0)
        k.op("dve", V.memset, [], ["upadz"], ap=upad[:, 15 + T:30 + T], constant=0.0)
        wAB, cAB = {}, {}

        def get_blocks(blk):
            if blk not in wAB:
                wAB[blk] = [wget(), None]
            return wAB[blk]

        def get_wB(blk):
            if wAB[blk][1] is None:
                wAB[blk][1] = wget()
            return wAB[blk][1]

        def get_conv(blk):
            if blk not in cAB:
                cAB[blk] = (wgetB(), wgetB())
            return cAB[blk]

        wA, kA = get_blocks(0)[0]
        lru_head(0, 0, wA, kA)
        conv_prep(0)
        for j in range(8):
            blk, jj = j // 2, j % 2
            gwv, gwk = load_gates(j)
            wB, kB = get_wB(blk)
            (cwA, ckA), (cwB, ckB) = get_conv(blk)
            lru_fronts(j, (0, 1), gwv, gwk)
            conv_A(j, jj, cwA, ckA, cwB, ckB, 0)
            conv_A(j, jj, cwA, ckA, cwB, ckB, 1)
            lru_sqrt_backs(j, (0, 1))
            conv_A(j, jj, cwA, ckA, cwB, ckB, 2)
            conv_A(j, jj, cwA, ckA, cwB, ckB, 3)
            if j < 7:
                conv_prep(j + 1)
            lru_fronts(j, (2, 3), gwv, gwk)
            conv_B(j, 0)
            conv_B(j, 1)
            if j < 7:
                wA2, kA2 = get_blocks((j + 1) // 2)[0]
                lru_head(j + 1, (j + 1) % 2, wA2, kA2)
            lru_sqrt_backs(j, (2, 3))
            lru_tail(j, jj, wB, kB)
            conv_B(j, 2)
            conv_B(j, 3)

        k.fence()
        RSv = lambda t: scr[0][:, tsl(t)]
        NBv = lambda t: scr[1][:, tsl(t)]
        sq2 = [scrb[2][:, 0:512], scrb[2][:, 512:1024]]
        SQ = lambda i: ("scr", 2, "q%d" % i)
        m2 = scr[3][:, 0:512]
        for t in range(NT):
            pss, pks = psum()
            psq, pkq = psum()
            for j in range(8):
                k.op("act", A.activation, [("yc", j, t)], [SQ(j % 2)], out=sq2[j % 2], in_=Ycv[:, j, tsl(t)], func=AF.Square)
                k.mm(P.matmul, [("yc", j, t), "ones"], [pks], True, out=pss, lhsT=ones_bf, rhs=Ycv[:, j, tsl(t)],
                     start=(j == 0), stop=(j == 7))
                k.mm(P.matmul, [SQ(j % 2), "ones"], [pkq], True, out=psq, lhsT=ones_bf, rhs=sq2[j % 2],
                     start=(j == 0), stop=(j == 7))
            k.op("dve", V.tensor_scalar, [pks], [("scr", 1, "nb%d" % t)], out=NBv(t), in0=pss, scalar1=1.0 / D, scalar2=None, op0=ALU.mult)
            k.op("dve", V.tensor_tensor, [("scr", 1, "nb%d" % t)], [("scr", 3, "m2")], out=m2, in0=NBv(t), in1=NBv(t), op=ALU.mult)
            k.op("dve", V.scalar_tensor_tensor, [pkq, ("scr", 3, "m2")], [("scr", 0, "rs%d" % t)], out=RSv(t), in0=psq, scalar=1.0 / D,
                 in1=m2, op0=ALU.mult, op1=ALU.subtract)
            k.op("dve", V.tensor_scalar, [("scr", 0, "rs%d" % t)], [("scr", 0, "rs%d" % t)], out=RSv(t), in0=RSv(t), scalar1=0.0,
                 scalar2=LN_EPS, op0=ALU.max, op1=ALU.add)
            k.op("act", A.activation, [("scr", 0, "rs%d" % t)], [("scr", 0, "rs%d" % t)], out=RSv(t), in_=RSv(t), func=AF.Sqrt)
            k.op("dve", V.reciprocal, [("scr", 0, "rs%d" % t)], [("scr", 0, "rs%d" % t)], out=RSv(t), in_=RSv(t))
            k.op("dve", V.scalar_tensor_tensor, [("scr", 1, "nb%d" % t), ("scr", 0, "rs%d" % t)], [("scr", 1, "nb%d" % t)],
                 out=NBv(t), in0=NBv(t), scalar=-1.0, in1=RSv(t), op0=ALU.mult, op1=ALU.mult)
        k.fence()
        k.retire(R2K + ALLK("x0"))
        tA = [scr[2][:, 1024:1536], scr[2][:, 1536:2048]]
        tB = [scr[3][:, 512:1024], scr[3][:, 1024:1536]]
        tC = [scr[4][:, 0:512], scr[4][:, 512:1024]]
        tD = [scr[4][:, 1024:1536], scr[4][:, 1536:2048]]
        it = 0
        for blk in range(4):
            wA, kA = wget()
            for jj in range(2):
                j = blk * 2 + jj
                for t in range(NT):
                    i2 = it % 2
                    it += 1
                    KA_, KB_, KC_, KD_ = ("scr", 2, "tA%d" % i2), ("scr", 3, "tB%d" % i2), ("scr", 4, "tC%d" % i2), ("scr", 4, "tD%d" % i2)
                    if SPLIT_RELOAD and SPLIT_Q0_INTERLEAVE and it <= 16:
                        tt_ = it - 1
                        if tt_ % 2 == 0:
                            stv, stk = scr[3][:, 1536:2048], ("scr", 3, "stg0")
                        else:
                            stv, stk = scr[2][:, 512:1024], ("scr", 2, "stg1")
                        reload_tile(bi, tt_, 0, stv, stk, "rst%d" % (tt_ % 2))
                    ps, pk = psum()
                    for kc in range(8):
                        k.mm(P.matmul, [kA, ("h0", kc, t)], [pk], kc == 7, out=ps, lhsT=wA[:, kc, jj * 128:(jj + 1) * 128],
                             rhs=H0v[:, kc, tsl(t)], start=(kc == 0), stop=(kc == 7))
                    k.op("act", A.activation, [pk], [KA_], out=tA[i2], in_=ps, func=AF.Silu)
                    k.op("dve", V.tensor_tensor, [("yc", j, t), ("scr", 0, "rs%d" % t)], [KB_], out=tB[i2], in0=Ycv[:, j, tsl(t)],
                         in1=RSv(t), op=ALU.mult)
                    k.op("dve", V.tensor_tensor, [KB_, ("scr", 1, "nb%d" % t)], [KB_], out=tB[i2], in0=tB[i2], in1=NBv(t), op=ALU.add)
                    k.op("act", A.activation, [KB_] + PR, [KB_], out=tB[i2], in_=tB[i2], func=AF.Identity,
                         scale=prmT[:, j, R_LG:R_LG + 1], bias=prmT[:, j, R_LB:R_LB + 1])
                    k.op("act", A.activation, [KB_], [KC_], out=tC[i2], in_=tB[i2], func=AF.Silu)
                    k.op("dve", V.tensor_tensor, [KC_, KA_], [("yc", j, t)], out=Ycv[:, j, tsl(t)], in0=tC[i2], in1=tA[i2], op=ALU.mult)
        k.fence()
        k.retire(ALLK("h0"))

        def hook_q1(step):
            stv = scr[step % 2][:, 0:512]
            reload_tile(bi, step, 1, stv, ("scr", step % 2), "xst%d" % (step % 2))

        if SPLIT_RELOAD and not SPLIT_Q0_INTERLEAVE:
            for tt_ in range(16):
                stv = scr[2 + tt_ % 2][:, 0:512]
                reload_tile(bi, tt_, 0, stv, ("scr", 2 + tt_ % 2), "rst%d" % (tt_ % 2))
        if not SPLIT_RELOAD:
            k.retire(R2K + ALLK("x0"))
            load_transposed(lambda tt: x_d[bi, tt * 128:(tt + 1) * 128, :], 16,
                            lambda h, tt: X1.q[h][:, :, tt * 128:(tt + 1) * 128], lambda kk, tt: X1.key(kk, tt // 4))
        out_proj(0, bi, X1, Ybv, "yb", mblks=range(0, 2), hook=hook_q1 if SPLIT_RELOAD else None)
        out_proj(0, bi, X1, Ybv, "yb", mblks=range(2, 4))
        k.fence()
        k.retire(ALLK("yb"))

        def h1_tile(t):
            modulated_norm(lambda kc: X1.c(kc, tsl(t)), lambda kc: X1.key(kc, t), 512, 1, bi,
                           lambda kc: H1v[:, kc, tsl(t)], lambda kc: ("h1", kc, t))

        out_proj_touter(0, bi, X1, Ycv, "yc", h1_tile)
        k.retire(ALLK("yc"))
        k.fence()
        Sb, Pb = scr[1], scr[2]
        KS, KP = ("scr", 1), ("scr", 2)
        tE = [scr[4][:, 0:512], scr[4][:, 512:1024]]
        ich = 0
        for hf in range(2):
            for blk in range(4):
                wA, kA = wget()
                for jj in range(2):
                    jl = blk * 2 + jj
                    g = (hf * 8 + jl) // 4
                    w = POOL_W[g]
                    hw_ = w // 2
                    Ub, KU = (scr[0], ("scr", 0)) if ich % 2 == 0 else (scr[3], ("scr", 3))
                    ich += 1
                    for t in range(NT):
                        ps, pk = psum()
                        for kc in range(8):
                            k.mm(P.matmul, [kA, ("h1", kc, t)], [pk], kc == 7, out=ps, lhsT=wA[:, kc, jj * 128:(jj + 1) * 128],
                                 rhs=H1v[:, kc, tsl(t)], start=(kc == 0), stop=(kc == 7))
                        k.op("act", A.copy, [pk], [KU], out=Ub[:, tsl(t)], in_=ps)
                    k.op("dve", V.memset, [], [KS], ap=Sb[:, 0:1], constant=0.0)
                    k.op("dve", V.tensor_tensor_scan, [KU, "misc"], [KS], out=Sb[:, 1:T + 1], data0=bc_ap(onecol, T), data1=Ub[:, 0:T],
                         initial=0.0, op0=ALU.mult, op1=ALU.add)
                    n_in = 64 - w + 1
                    k.op("dve", V.tensor_tensor, [KS], [KP], out=rows_ap(Pb, hw_, n_in), in0=rows_ap(Sb, w, n_in),
                         in1=rows_ap(Sb, 0, n_in), op=ALU.subtract)
                    k.op("dve", V.tensor_tensor, [KS], [KP], out=rows_ap(Pb, 0, hw_), in0=rows_ap(Sb, hw_, hw_),
                         in1=rows_ap(Sb, 0, hw_, bcast=True), op=ALU.subtract)
                    icl = wic[:, g, 0:1]
                    k.op("dve", V.tensor_tensor, [KP, "icnt"], [KP], out=rows_ap(Pb, 0, hw_), in0=rows_ap(Pb, 0, hw_),
                         in1=bass.AP(icl.tensor, icl.offset, [list(icl.ap[0]), [0, 32], [1, hw_]]), op=ALU.mult)
                    if hw_ > 1:
                        k.op("dve", V.tensor_tensor, [KS], [KP], out=rows_ap(Pb, 64 - hw_ + 1, hw_ - 1),
                             in0=rows_ap(Sb, 64, hw_ - 1, bcast=True), in1=rows_ap(Sb, 65 - w, hw_ - 1), op=ALU.subtract)
                        icr = wic[:, g, 64 - hw_ + 1:64 - hw_ + 2]
                        k.op("dve", V.tensor_tensor, [KP, "icnt"], [KP], out=rows_ap(Pb, 64 - hw_ + 1, hw_ - 1),
                             in0=rows_ap(Pb, 64 - hw_ + 1, hw_ - 1),
                             in1=bass.AP(icr.tensor, icr.offset, [list(icr.ap[0]), [0, 32], [1, hw_ - 1]]), op=ALU.mult)
                    k.op("dve", V.scalar_tensor_tensor, [KP, KU], [("D", jl, t) for t in range(NT)], out=Ycv[:, jl, :], in0=Pb[:, 0:T],
                         scalar=1.0 / w, in1=Ub[:, 0:T], op0=ALU.mult, op1=ALU.subtract)
            for gl in range(2):
                g = hf * 2 + gl
                wG, kG = wget()
                wg1, kg1 = wget()
                wg2, kg2 = wget()
                for t in range(NT):
                    pys = []
                    for mo in range(4):
                        ps, pk = psum()
                        for ki in range(4):
                            k.mm(P.matmul, [kG, ("D", gl * 4 + ki, t)], [pk], ki == 3, out=ps, lhsT=wG[:, ki, mo * 128:(mo + 1) * 128],
                                 rhs=Ycv[:, gl * 4 + ki, tsl(t)], start=(ki == 0), stop=(ki == 3))
                        pys.append((ps, pk))
                    for mo in range(4):
                        wgv, kgv = (wg1, kg1) if mo < 2 else (wg2, kg2)
                        ps, pk = psum()
                        for kc in range(8):
                            k.mm(P.matmul, [kgv, ("h1", kc, t)], [pk], kc == 7, out=ps, lhsT=wgv[:, kc, (mo % 2) * 128:(mo % 2 + 1) * 128],
                                 rhs=H1v[:, kc, tsl(t)], start=(kc == 0), stop=(kc == 7))
                        KE = ("scr", 4, "tE%d" % (mo % 2))
                        k.op("act", A.activation, [pk], [KE], out=tE[mo % 2], in_=ps, func=AF.Tanh, scale=0.5)
                        k.op("dve", V.scalar_tensor_tensor, [KE, pk], [KE], out=tE[mo % 2], in0=tE[mo % 2], scalar=1.0, in1=ps,
                             op0=ALU.add, op1=ALU.mult)
                        psy, pky = pys[mo]
                        k.op("dve", V.scalar_tensor_tensor, [pky, KE] + LC, [("D", gl * 4 + mo, t)], out=Ycv[:, gl * 4 + mo, tsl(t)], in0=psy,
                             scalar=hsc[:, (g * 4 + mo) // 8, (g * 4 + mo) % 8:(g * 4 + mo) % 8 + 1], in1=tE[mo % 2],
                             op0=ALU.mult, op1=ALU.mult)
            if hf == 0:
                out_proj(1, bi, X1, Ycv, "D")

        k.fence()
        fgc = lambda kc: prmT[:, kc, R_FG:R_FG + 1]

        def final_tile(t):
            rs = rs_t
            rstd_tile(lambda kc: X1.c(kc, tsl(t)), lambda kc: X1.key(kc, t), 512, rs)
            nrm = [scr[kc // 4][:, (kc % 4) * 512:(kc % 4 + 1) * 512] for kc in range(8)]
            NK = lambda kc: ("scr", kc // 4, "n%d" % kc)
            for kc in range(8):
                k.op("dve", V.scalar_tensor_tensor, [X1.key(kc, t), SM("rs")] + PR, [NK(kc)], out=nrm[kc], in0=X1.c(kc, tsl(t)),
                     scalar=fgc(kc), in1=rs, op0=ALU.mult, op1=ALU.mult)
            for q in range(4):
                oi = q % 2
                ost = scr[2 + oi][:, 0:D]
                OK = ("scr", 2 + oi)
                for h in range(2):
                    ps, pk = psum()
                    for qq in range(4):
                        kc = h * 4 + qq
                        k.mm(P.transpose, [NK(kc), "ident"], [pk], qq == 3, out=ps[:, qq * 128:(qq + 1) * 128],
                             in_=nrm[kc][:, q * 128:(q + 1) * 128], identity=ident)
                    k.op("act", A.copy, [pk], [OK], out=ost[:, h * 512:(h + 1) * 512], in_=ps)
                r0 = t * 512 + q * 128
                k.dma("act", "ost%d" % oi, out=out_d[bi, r0:r0 + 128, :], in_=ost, reads=[OK], writes=[])

        out_proj_touter(1, bi, X1, Ycv, "D", final_tile)

    assert wq["used"] == len(plan) and wqB["used"] == len(planB), (wq["used"], len(plan), wqB["used"], len(planB))
    for nm in ("ost0", "ost1"):
        sem = k.dsem[nm]
        nc.sync.wait_ge(sem, k.dcnt[sem.num])
        nc.scalar.wait_ge(sem, k.dcnt[sem.num])
    return nc


_NC_CACHE = {}


def _prm_rows(inp, core):
    b0 = core * NB
    rows = np.zeros((NROW, D), np.float32)
    rows[R_C:R_C + 4] = inp["c"][b0:b0 + 4]
    rows[R_CCTX] = inp["c_ctx"]
    rows[R_NG:R_NG + 2] = inp["norm_g"]
    rows[R_MODB:R_MODB + 6] = inp["mod_b"].reshape(6, D)
    rows[R_CW:R_CW + 31] = inp["ev_conv_w"][0]
    rows[R_CB] = inp["ev_conv_b"][0]
    rows[R_LG] = inp["ev_ln_g"][0]
    rows[R_LB] = inp["ev_ln_b"][0]
    rows[R_SW:R_SW + 4] = inp["ev_sconv_w"][0]
    rows[R_SB] = inp["ev_sconv_b"][0]
    rows[R_BR:R_BR + 2] = inp["ev_b_r"][0]
    rows[R_BI:R_BI + 2] = inp["ev_b_i"][0]
    rows[R_LAM:R_LAM + 2] = inp["ev_lam"][0]
    rows[R_OS:R_OS + 2] = inp["od_scale"][0].reshape(2, D)
    rows[R_FG] = inp["final_g"]
    return rows


def kernel(**inp):
    inp = {k_: np.asarray(v) for k_, v in inp.items()}
    if "nc" not in _NC_CACHE:
        _NC_CACHE["nc"] = build_program()
    nc = _NC_CACHE["nc"]
    shared = {
        "mod_w": np.ascontiguousarray(inp["mod_w"], np.float32),
        "ev_w_in": np.ascontiguousarray(inp["ev_w_in"][0], np.float32),
        "ev_w_ri": np.ascontiguousarray(np.stack([inp["ev_w_r"][0], inp["ev_w_i"][0]]), np.float32),
        "ev_w_out": np.ascontiguousarray(inp["ev_w_out"][0], np.float32),
        "od_w_in": np.ascontiguousarray(inp["od_w_in"][0], np.float32),
        "od_w_grp": np.ascontiguousarray(inp["od_w_grp"][0], np.float32),
        "od_w_out": np.ascontiguousarray(inp["od_w_out"][0], np.float32),
    }
    in_maps = []
    for c in range(8):
        m = dict(shared)
        m["x"] = np.ascontiguousarray(inp["x"][c * NB:(c + 1) * NB], np.float32)
        m["ctx"] = np.ascontiguousarray(inp["ctx"][c * NB:(c + 1) * NB], np.float32)
        m["prm"] = _prm_rows(inp, c)
        in_maps.append(m)
    res = run_bass_kernel_spmd(nc, in_maps, core_ids=list(range(8)))
    return np.concatenate([r["out"] for r in res.results], axis=0)
```
